# Optimizing a Trainium2 kernel written in Bass

```python
import math
import jax, jax.numpy as jnp
from jax import lax
import numpy as np

D_MODEL = 1024
BATCH = 32
SEQ = 256
DEPTH = 4
DEC_BATCH = 4
DEC_SEQ = 2048
PAST_LEN = 512

GRID_W = 64
N_S5_LAYERS = (DEPTH + 1) // 2
N_RET_LAYERS = DEPTH // 2
S5_WIDTH = D_MODEL // 2
CONV_WIDTH = D_MODEL - S5_WIDTH
S5_GROUP_CH = 16
S5_GROUPS = S5_WIDTH // S5_GROUP_CH
S5_STATE = 64
CONV_K = 3
HY_IN = S5_WIDTH + 3 * CONV_WIDTH
HY_MIX = S5_WIDTH + CONV_WIDTH
RET_HEADS = 8
RET_DK = D_MODEL // RET_HEADS
RET_DV = 2 * RET_DK
RET_QK = RET_HEADS * RET_DK
RET_V = RET_HEADS * RET_DV
RET_IN = 2 * RET_QK + 2 * RET_V
RET_CHUNK = 128
ROPE_BASE = 10000.0
MLP_HIDDEN = 4 * D_MODEL
N_MOD = 6
EPS = 1e-6

kernel_name = 'hybrid_s5_conv_retention_diffusion_step'

F32 = jnp.float32


def rms_norm(x, w):
    xf = x.astype(F32)
    y = xf * lax.rsqrt(jnp.mean(xf * xf, axis=-1, keepdims=True) + EPS)
    return (y * w.astype(F32)).astype(x.dtype)


def _cmul(ar, ai, br, bi):
    return ar * br - ai * bi, ar * bi + ai * br


def s5_direction(ug, lam_re, lam_im, log_step, b_re, b_im, c_re, c_im, h0, reverse):
    dt = jnp.exp(log_step.astype(F32))[:, None]
    lam_re = lam_re.astype(F32)
    lam_im = lam_im.astype(F32)
    mag = jnp.exp(lam_re * dt)
    ab_re, ab_im = mag * jnp.cos(lam_im * dt), mag * jnp.sin(lam_im * dt)
    den = lam_re * lam_re + lam_im * lam_im
    f_re = ((ab_re - 1.0) * lam_re + ab_im * lam_im) / den
    f_im = (ab_im * lam_re - (ab_re - 1.0) * lam_im) / den
    bb_re, bb_im = _cmul(f_re[..., None], f_im[..., None], b_re.astype(F32), b_im.astype(F32))
    bu_re = jnp.einsum('blgc,gpc->blgp', ug, bb_re)
    bu_im = jnp.einsum('blgc,gpc->blgp', ug, bb_im)
    if h0 is not None:
        h_re, h_im = _cmul(ab_re, ab_im, h0[0], h0[1])
        first = -1 if reverse else 0
        bu_re = bu_re.at[:, first].add(h_re)
        bu_im = bu_im.at[:, first].add(h_im)
    a_re = jnp.broadcast_to(ab_re, bu_re.shape)
    a_im = jnp.broadcast_to(ab_im, bu_im.shape)

    def combine(e1, e2):
        a1r, a1i, b1r, b1i = e1
        a2r, a2i, b2r, b2i = e2
        ar, ai = _cmul(a2r, a2i, a1r, a1i)
        br, bi = _cmul(a2r, a2i, b1r, b1i)
        return ar, ai, br + b2r, bi + b2i

    _, _, x_re, x_im = lax.associative_scan(combine, (a_re, a_im, bu_re, bu_im), axis=1, reverse=reverse)
    y = (jnp.einsum('blgp,gcp->blgc', x_re, c_re.astype(F32))
         - jnp.einsum('blgp,gcp->blgc', x_im, c_im.astype(F32)))
    last = 0 if reverse else -1
    return y, x_re[:, last], x_im[:, last]


def s5_mixer(u, lam_re, lam_im, log_step, b_re, b_im, c_re, c_im, d, h0_re, h0_im):
    bsz, length, _ = u.shape
    uf = u.astype(F32)
    ug = uf.reshape(bsz, length, S5_GROUPS, S5_GROUP_CH)
    y = d.astype(F32) * uf
    fin_re, fin_im = [], []
    for direction in range(2):
        h0 = None if h0_re is None else (h0_re[:, direction].astype(F32), h0_im[:, direction].astype(F32))
        yd, fr, fi = s5_direction(ug, lam_re[direction], lam_im[direction], log_step[direction],
                                  b_re[direction], b_im[direction], c_re[direction], c_im[direction],
                                  h0, reverse=(direction == 1))
        y = y + yd.reshape(bsz, length, S5_WIDTH)
        fin_re.append(fr)
        fin_im.append(fi)
    return y, jnp.stack(fin_re, 1), jnp.stack(fin_im, 1)


def short_conv(h, w, b):
    out = lax.conv_general_dilated(h, w[:, None, :].astype(h.dtype), window_strides=(1,),
                                   padding=((CONV_K // 2, CONV_K // 2),),
                                   dimension_numbers=('NWC', 'WIO', 'NWC'),
                                   feature_group_count=h.shape[-1])
    return out + b.astype(h.dtype)


def hybrid_mixer(h, in_w, out_w, lam_re, lam_im, log_step, b_re, b_im, c_re, c_im, d,
                 glu_w, glu_b, conv_w, conv_b, h0_re, h0_im):
    proj = h @ in_w
    u, bg, cg, v = jnp.split(proj, [S5_WIDTH, S5_WIDTH + CONV_WIDTH, S5_WIDTH + 2 * CONV_WIDTH], axis=-1)
    y, s_re, s_im = s5_mixer(u, lam_re, lam_im, log_step, b_re, b_im, c_re, c_im, d, h0_re, h0_im)
    z = jax.nn.gelu(y)
    a_out = z * jax.nn.sigmoid(z @ glu_w.astype(F32) + glu_b.astype(F32))
    b_out = bg * short_conv(cg * v, conv_w, conv_b)
    mixed = jnp.concatenate([a_out.astype(b_out.dtype), b_out], axis=-1)
    return mixed @ out_w, s_re, s_im


def axial_rotary(t):
    length = t.shape[1]
    rows = length // GRID_W
    row = jnp.repeat(jnp.arange(rows, dtype=F32), GRID_W)
    col = jnp.tile(jnp.arange(GRID_W, dtype=F32), rows)
    n_freq = RET_DK // 4
    inv_freq = jnp.power(ROPE_BASE, -jnp.arange(n_freq, dtype=F32) / n_freq)

    def rot(x, pos):
        ang = pos[:, None] * inv_freq
        cos = jnp.cos(ang)[:, None, :]
        sin = jnp.sin(ang)[:, None, :]
        x1, x2 = x[..., :n_freq], x[..., n_freq:]
        return jnp.concatenate([x1 * cos - x2 * sin, x1 * sin + x2 * cos], axis=-1)

    half = RET_DK // 2
    return jnp.concatenate([rot(t[..., :half], row), rot(t[..., half:], col)], axis=-1)


def retention_chunked(q, k, v, log_g, s0):
    bsz, length, _, _ = q.shape
    n_chunks = length // RET_CHUNK
    idx = jnp.arange(RET_CHUNK, dtype=F32)
    diff = idx[:, None] - idx[None, :]
    intra = jnp.where(diff >= 0, jnp.exp(log_g[:, None, None] * jnp.maximum(diff, 0.0)), 0.0)
    q_decay = jnp.exp(log_g[None, :] * (idx[:, None] + 1.0))[None, :, :, None]
    k_decay = jnp.exp(log_g[None, :] * (RET_CHUNK - 1.0 - idx[:, None]))[None, :, :, None]
    chunk_decay = jnp.exp(log_g * RET_CHUNK)[None, :, None, None]
    if s0 is None:
        s0 = jnp.zeros((bsz, RET_HEADS, RET_DK, RET_DV), F32)

    def to_chunks(t):
        return jnp.moveaxis(t.reshape(bsz, n_chunks, RET_CHUNK, t.shape[2], t.shape[3]), 1, 0)

    def step(s, qkv):
        qc, kc, vc = qkv
        scores = jnp.einsum('bihd,bjhd->bhij', qc, kc) * intra
        o = (jnp.einsum('bhij,bjhe->bihe', scores, vc)
             + jnp.einsum('bihd,bhde->bihe', qc, s) * q_decay)
        s = s * chunk_decay + jnp.einsum('bjhd,bjhe->bhde', kc * k_decay, vc)
        return s, o

    s_fin, o = lax.scan(step, s0, (to_chunks(q), to_chunks(k), to_chunks(v)))
    o = jnp.moveaxis(o, 0, 1).reshape(bsz, length, RET_HEADS, RET_DV)
    return o, s_fin


def retention_mixer(h, in_w, out_w, gamma_logit, gn_w, s0):
    latent = s0 is not None
    bsz, length, _ = h.shape
    proj = h @ in_w
    q, k, v, g = jnp.split(proj, [RET_QK, 2 * RET_QK, 2 * RET_QK + RET_V], axis=-1)
    q = q.astype(F32).reshape(bsz, length, RET_HEADS, RET_DK)
    k = k.astype(F32).reshape(bsz, length, RET_HEADS, RET_DK)
    v = v.astype(F32).reshape(bsz, length, RET_HEADS, RET_DV)
    if latent:
        q = axial_rotary(q)
        k = axial_rotary(k)
    q = q * (RET_DK ** -0.5)
    log_g = jax.nn.log_sigmoid(gamma_logit.astype(F32))
    s0_f = s0[:, 0].astype(F32) if latent else None
    s0_b = s0[:, 1].astype(F32) if latent else None
    o_f, s_f = retention_chunked(q, k, v, log_g[0], s0_f)
    o_b, s_b = retention_chunked(jnp.flip(q, 1), jnp.flip(k, 1), jnp.flip(v, 1), log_g[1], s0_b)
    o = o_f + jnp.flip(o_b, 1)
    o = o * lax.rsqrt(jnp.mean(o * o, axis=-1, keepdims=True) + EPS) * gn_w.astype(F32).reshape(RET_HEADS, RET_DV)
    o = jax.nn.silu(g.astype(F32)) * o.reshape(bsz, length, RET_V)
    return o.astype(h.dtype) @ out_w, jnp.stack([s_f, s_b], axis=1)


def squared_relu_mlp(h, w1, w2):
    a = jax.nn.relu(h @ w1)
    return (a * a) @ w2


def run_trunk(x, cond, prm, s5_re0, s5_im0, ret0):
    latent = ret0 is not None
    sc = jax.nn.silu(cond)
    new_re, new_im, new_ret = [], [], []
    for i in range(DEPTH):
        j = i // 2
        mod = sc @ prm['ada_w'][i] + prm['ada_b'][i]
        sh1, sc1, g1, sh2, sc2, g2 = jnp.split(mod[:, None, :], N_MOD, axis=-1)
        h = rms_norm(x, prm['norm1_w'][i]) * (1.0 + sc1) + sh1
        if i % 2 == 0:
            out, s_re, s_im = hybrid_mixer(
                h, prm['hy_in_w'][j], prm['hy_out_w'][j],
                prm['s5_lam_re'][j], prm['s5_lam_im'][j], prm['s5_log_step'][j],
                prm['s5_b_re'][j], prm['s5_b_im'][j], prm['s5_c_re'][j], prm['s5_c_im'][j],
                prm['s5_d'][j], prm['s5_glu_w'][j], prm['s5_glu_b'][j],
                prm['conv_w'][j], prm['conv_b'][j],
                s5_re0[:, j] if latent else None, s5_im0[:, j] if latent else None)
            new_re.append(s_re)
            new_im.append(s_im)
        else:
            out, s_ret = retention_mixer(h, prm['ret_in_w'][j], prm['ret_out_w'][j],
                                         prm['ret_gamma_logit'][j], prm['ret_gn_w'][j],
                                         ret0[:, j] if latent else None)
            new_ret.append(s_ret)
        x = x + g1 * out
        h = rms_norm(x, prm['norm2_w'][i]) * (1.0 + sc2) + sh2
        x = x + g2 * squared_relu_mlp(h, prm['mlp_w1'][i], prm['mlp_w2'][i])
    y = rms_norm(x, prm['final_norm_w'])
    if latent:
        return y
    return y, jnp.stack(new_re, 1), jnp.stack(new_im, 1), jnp.stack(new_ret, 1)


def setup_inputs(seed: int = 0) -> dict:
    key = jax.random.key(seed)
    ks = jax.random.split(key, 32)
    D = D_MODEL

    def nrm(k, shape, s):
        return jax.random.normal(k, shape, F32) * s

    gam = 1.0 - jnp.power(2.0, -5.0 - jnp.arange(RET_HEADS, dtype=F32))
    ret_scale = jnp.sqrt((1.0 - gam ** (2 * PAST_LEN)) / (1.0 - gam * gam))
    s5_shape = (N_S5_LAYERS, 2, S5_GROUPS, S5_STATE)
    return {
        'x_prompt': nrm(ks[0], (BATCH, SEQ, D), 1.0),
        'x_sample': nrm(ks[1], (DEC_BATCH, DEC_SEQ, D), 1.0),
        'state_s5_re': nrm(ks[2], (DEC_BATCH,) + s5_shape, 1.0),
        'state_s5_im': nrm(ks[3], (DEC_BATCH,) + s5_shape, 1.0),
        'state_ret': nrm(ks[4], (DEC_BATCH, N_RET_LAYERS, 2, RET_HEADS, RET_DK, RET_DV), 1.0) * ret_scale[:, None, None],
        'c': nrm(ks[5], (DEC_BATCH, D), 1.0),
        'c_ctx': nrm(ks[6], (D,), 1.0),
        'norm1_w': 1.0 + nrm(ks[7], (DEPTH, D), 0.02),
        'norm2_w': 1.0 + nrm(ks[8], (DEPTH, D), 0.02),
        'ada_w': nrm(ks[9], (DEPTH, D, N_MOD * D), 0.5 * D ** -0.5),
        'ada_b': nrm(ks[10], (DEPTH, N_MOD * D), 0.02),
        'hy_in_w': nrm(ks[11], (N_S5_LAYERS, D, HY_IN), D ** -0.5),
        'hy_out_w': nrm(ks[12], (N_S5_LAYERS, HY_MIX, D), HY_MIX ** -0.5),
        's5_lam_re': -0.5 + nrm(ks[13], s5_shape, 0.01),
        's5_lam_im': math.pi * jnp.arange(S5_STATE, dtype=F32) + nrm(ks[14], s5_shape, 0.01),
        's5_log_step': jax.random.uniform(ks[15], (N_S5_LAYERS, 2, S5_GROUPS), F32, math.log(1e-3), math.log(1e-1)),
        's5_b_re': nrm(ks[16], s5_shape + (S5_GROUP_CH,), (2 * S5_GROUP_CH) ** -0.5),
        's5_b_im': nrm(ks[17], s5_shape + (S5_GROUP_CH,), (2 * S5_GROUP_CH) ** -0.5),
        's5_c_re': nrm(ks[18], (N_S5_LAYERS, 2, S5_GROUPS, S5_GROUP_CH, S5_STATE), (2 * S5_STATE) ** -0.5),
        's5_c_im': nrm(ks[19], (N_S5_LAYERS, 2, S5_GROUPS, S5_GROUP_CH, S5_STATE), (2 * S5_STATE) ** -0.5),
        's5_d': nrm(ks[20], (N_S5_LAYERS, S5_WIDTH), 1.0),
        's5_glu_w': nrm(ks[21], (N_S5_LAYERS, S5_WIDTH, S5_WIDTH), S5_WIDTH ** -0.5),
        's5_glu_b': nrm(ks[22], (N_S5_LAYERS, S5_WIDTH), 0.02),
        'conv_w': nrm(ks[23], (N_S5_LAYERS, CONV_K, CONV_WIDTH), CONV_K ** -0.5),
        'conv_b': nrm(ks[24], (N_S5_LAYERS, CONV_WIDTH), 0.02),
        'ret_in_w': nrm(ks[25], (N_RET_LAYERS, D, RET_IN), D ** -0.5),
        'ret_out_w': nrm(ks[26], (N_RET_LAYERS, RET_V, D), RET_V ** -0.5),
        'ret_gamma_logit': jnp.log(gam / (1.0 - gam)) + nrm(ks[27], (N_RET_LAYERS, 2, RET_HEADS), 0.05),
        'ret_gn_w': 1.0 + nrm(ks[28], (N_RET_LAYERS, RET_V), 0.02),
        'mlp_w1': nrm(ks[29], (DEPTH, D, MLP_HIDDEN), D ** -0.5),
        'mlp_w2': nrm(ks[30], (DEPTH, MLP_HIDDEN, D), MLP_HIDDEN ** -0.5),
        'final_norm_w': 1.0 + nrm(ks[31], (D,), 0.02),
    }


def reference(x_prompt, x_sample, state_s5_re, state_s5_im, state_ret, c, c_ctx,
              norm1_w, norm2_w, ada_w, ada_b, hy_in_w, hy_out_w,
              s5_lam_re, s5_lam_im, s5_log_step, s5_b_re, s5_b_im, s5_c_re, s5_c_im,
              s5_d, s5_glu_w, s5_glu_b, conv_w, conv_b,
              ret_in_w, ret_out_w, ret_gamma_logit, ret_gn_w,
              mlp_w1, mlp_w2, final_norm_w):
    prm = dict(norm1_w=norm1_w, norm2_w=norm2_w, ada_w=ada_w, ada_b=ada_b,
               hy_in_w=hy_in_w, hy_out_w=hy_out_w,
               s5_lam_re=s5_lam_re, s5_lam_im=s5_lam_im, s5_log_step=s5_log_step,
               s5_b_re=s5_b_re, s5_b_im=s5_b_im, s5_c_re=s5_c_re, s5_c_im=s5_c_im,
               s5_d=s5_d, s5_glu_w=s5_glu_w, s5_glu_b=s5_glu_b,
               conv_w=conv_w, conv_b=conv_b,
               ret_in_w=ret_in_w, ret_out_w=ret_out_w, ret_gamma_logit=ret_gamma_logit, ret_gn_w=ret_gn_w,
               mlp_w1=mlp_w1, mlp_w2=mlp_w2, final_norm_w=final_norm_w)
    y_prompt, new_s5_re, new_s5_im, new_ret = run_trunk(x_prompt, c_ctx[None, :], prm, None, None, None)
    y_sample = run_trunk(x_sample, c, prm, state_s5_re, state_s5_im, state_ret)
    return (y_prompt, y_sample, new_s5_re, new_s5_im, new_ret)
```

```python
import math
from contextlib import ExitStack
import numpy as np
import concourse.bass as bass
import concourse.mybir as mybir
from concourse.bass_utils import run_bass_kernel_spmd

F32 = mybir.dt.float32
BF16 = mybir.dt.bfloat16
ALU = mybir.AluOpType
AF = mybir.ActivationFunctionType

TOK = 2048
NSEG = 8
SEG = 256
EPS = 1e-6
MAGIC = 12582912.0
TWO_PI = 2.0 * math.pi


def _esz(dt):
    return 2 if dt == BF16 else 4


class Prog:
    def __init__(self, nc, self_sync=True):
        self.nc = nc
        self.ops = []
        self.hist = {}
        self.self_sync = self_sync
        self.out_groups = set()

    def _region(self, ap):
        t = ap.tensor
        tn = type(t).__name__
        if 'DRam' in tn:
            return None
        key = 'PSUM' if 'PSum' in tn else 'SBUF:' + t.name
        pairs = ap.ap
        off = int(ap.offset)
        pstep, pcnt = pairs[0]
        es = _esz(ap.dtype)
        if pstep == 0:
            p0 = 0
            rem = off
        else:
            p0 = off // pstep
            rem = off - p0 * pstep
        lo = 0
        hi = 0
        for st, cnt in pairs[1:]:
            d = st * (cnt - 1)
            if d < 0:
                lo += d
            else:
                hi += d
        b0 = (rem + lo) * es
        b1 = (rem + hi + 1) * es
        if key == 'PSUM':
            return key, 0, 128, b0 // 2048 * 2048, (b1 + 2047) // 2048 * 2048
        return key, p0, p0 + pcnt, b0, b1

    BUCKET = 2048

    def _access(self, ap, opid, is_w, eng, is_dma, mutate=True):
        r = self._region(ap)
        deps = set()
        if r is None:
            return deps
        key, p0, p1, b0, b1 = r
        hk = self.hist.setdefault(key, {})
        ent = (p0, p1, b0, b1, opid, is_w, eng, is_dma)
        for bk in range(b0 // self.BUCKET, (b1 - 1) // self.BUCKET + 1):
            h = hk.get(bk)
            if h is None:
                if mutate:
                    hk[bk] = [ent]
                continue
            keep = []
            for e in h:
                ep0, ep1, eb0, eb1, eid, ew, eeng, edma = e
                if eid == opid:
                    keep.append(e)
                    continue
                ov = not (ep1 <= p0 or p1 <= ep0 or eb1 <= b0 or b1 <= eb0)
                if ov and (is_w or ew or (key == 'PSUM' and eeng != eng)):
                    if not (is_w and ew and eng == 'pe' and eeng == 'pe'):
                        deps.add(eid)
                if not mutate:
                    continue
                contained = ep0 >= p0 and ep1 <= p1 and eb0 >= b0 and eb1 <= b1
                if is_w and contained:
                    continue
                if (not is_w) and (not ew) and contained and eeng == eng and not edma and not is_dma:
                    continue
                keep.append(e)
            if mutate:
                keep.append(ent)
                hk[bk] = keep
        return deps

    def add(self, eng, fn, reads=(), writes=(), dma=None):
        opid = len(self.ops)
        deps = set()
        isd = dma is not None
        for ap in reads:
            deps |= self._access(ap, opid, False, eng, isd, mutate=False)
        for ap in writes:
            deps |= self._access(ap, opid, True, eng, isd, mutate=False)
        for ap in reads:
            self._access(ap, opid, False, eng, isd, mutate=True)
        for ap in writes:
            self._access(ap, opid, True, eng, isd, mutate=True)
        self.ops.append(dict(eng=eng, fn=fn, deps=deps, dma=dma))
        return opid

    def _e(self, eng):
        nc = self.nc
        return {'pe': nc.tensor, 'act': nc.scalar, 'dve': nc.vector, 'pool': nc.gpsimd, 'sp': nc.sync}[eng]

    def mm(self, out, lhsT, rhs, start=True, stop=True, sgc=False):
        if sgc:
            self.add('pe', lambda: self.nc.tensor.matmul(out, lhsT, rhs, start=start, stop=stop,
                                                         skip_group_check=True),
                     reads=[lhsT, rhs], writes=[out])
        else:
            self.add('pe', lambda: self.nc.tensor.matmul(out, lhsT, rhs, start=start, stop=stop),
                     reads=[lhsT, rhs], writes=[out])

    def rsum(self, out, in_):
        self.add('dve', lambda: self.nc.vector.reduce_sum(out, in_, axis=mybir.AxisListType.X),
                 reads=[in_], writes=[out])

    def transpose(self, out, in_, ident):
        self.add('pe', lambda: self.nc.tensor.transpose(out, in_, ident), reads=[in_, ident], writes=[out])

    def act(self, out, in_, func, bias=None, scale=None, accum_out=None):
        kw = {}
        rd = [in_]
        wr = [out]
        if bias is not None:
            kw['bias'] = bias
            if not isinstance(bias, (int, float)):
                rd.append(bias)
        if scale is not None:
            kw['scale'] = scale
            if not isinstance(scale, (int, float)):
                rd.append(scale)
        if accum_out is not None:
            kw['accum_out'] = accum_out
            wr.append(accum_out)
        self.add('act', lambda: self.nc.scalar.activation(out, in_, func, **kw), reads=rd, writes=wr)

    def tt(self, eng, out, in0, in1, op):
        self.add(eng, lambda: self._e(eng).tensor_tensor(out, in0, in1, op), reads=[in0, in1], writes=[out])

    def ts(self, eng, out, in0, s1, op0, s2=None, op1=None):
        rd = [in0] + [s for s in (s1, s2) if s is not None and not isinstance(s, (int, float))]
        kw = {}
        if op1 is not None:
            kw['op1'] = op1
        self.add(eng, lambda: self._e(eng).tensor_scalar(out, in0, s1, s2, op0, **kw), reads=rd, writes=[out])

    def stt(self, eng, out, in0, scalar, in1, op0, op1):
        rd = [in0, in1] + ([scalar] if not isinstance(scalar, (int, float)) else [])
        self.add(eng, lambda: self._e(eng).scalar_tensor_tensor(out, in0, scalar, in1, op0, op1),
                 reads=rd, writes=[out])

    def scan(self, eng, out, d0, d1, init):
        rd = [d0, d1] + ([init] if not isinstance(init, (int, float)) else [])
        self.add(eng, lambda: self._e(eng).tensor_tensor_scan(out, d0, d1, init, ALU.mult, ALU.add),
                 reads=rd, writes=[out])

    def copy(self, eng, out, in_):
        if eng == 'act':
            return self.act(out, in_, AF.Copy)
        self.add(eng, lambda: self._e(eng).tensor_copy(out, in_), reads=[in_], writes=[out])

    def recip(self, out, in_):
        self.add('dve', lambda: self.nc.vector.reciprocal(out, in_), reads=[in_], writes=[out])

    def memset(self, eng, out, val):
        self.add(eng, lambda: self._e(eng).memset(out, val), reads=[], writes=[out])

    def dma(self, q, out, in_, group, is_out=False):
        if is_out:
            self.out_groups.add(group)
        self.add(q, lambda: self._e(q).dma_start(out, in_), reads=[in_], writes=[out], dma=group)

    def emit(self, stack):
        nc = self.nc
        ops = self.ops
        needed = set()
        for o in ops:
            needed |= o['deps']
        engs = ['pe', 'act', 'dve', 'pool', 'sp']
        sems = {e: stack.enter_context(nc.semaphore('s_' + e)) for e in engs}
        groups = sorted({o['dma'] for o in ops if o['dma'] is not None})
        gsem = {g: stack.enter_context(nc.semaphore('g_' + g)) for g in groups}
        cnt = {e: 0 for e in engs}
        gcnt = {g: 0 for g in groups}
        gcnt_at = []
        sig = {}
        for i, o in enumerate(ops):
            gcnt_at.append(dict(gcnt))
            if o['dma'] is not None:
                gcnt[o['dma']] += 16
                sig[i] = ('g', o['dma'], gcnt[o['dma']])
            elif i in needed:
                cnt[o['eng']] += 1
                sig[i] = ('e', o['eng'], cnt[o['eng']])
        per_eng = {e: [] for e in engs}
        for i, o in enumerate(ops):
            per_eng[o['eng']].append(i)
        self.stats = {e: len(per_eng[e]) for e in engs}
        self.stats.update({'sig_' + e: cnt[e] for e in engs})
        final_g = dict(gcnt)
        block = stack.enter_context(nc.Block())

        def run(e, handle):
            waited = {}
            nw = 0
            for i in per_eng[e]:
                o = ops[i]
                want = {}
                for d in o['deps']:
                    kind, who, val = sig[d]
                    if kind == 'g':
                        val = gcnt_at[i][who]
                        k = ('g', who)
                    else:
                        if who == e and not self.self_sync and o['dma'] is None:
                            continue
                        k = ('e', who)
                    if val > want.get(k, 0):
                        want[k] = val
                for k, val in want.items():
                    if waited.get(k, 0) >= val:
                        continue
                    waited[k] = val
                    s = gsem[k[1]] if k[0] == 'g' else sems[k[1]]
                    handle.wait_ge(s, val)
                    nw += 1
                ins = o['fn']()
                if i in sig:
                    kind, who, val = sig[i]
                    if kind == 'g':
                        ins.then_inc(gsem[who], 16)
                    else:
                        ins.then_inc(sems[who], 1)
            if e == 'sp':
                for g in sorted(self.out_groups):
                    handle.wait_ge(gsem[g], final_g[g])
            self.stats['w_' + e] = nw

        @block.tensor
        def _(h):
            run('pe', h)

        @block.scalar
        def _(h):
            run('act', h)

        @block.vector
        def _(h):
            run('dve', h)

        @block.gpsimd
        def _(h):
            run('pool', h)

        @block.sync
        def _(h):
            run('sp', h)


class Rec:
    def __init__(self):
        self.calls = []

    def __getattr__(self, name):
        def f(*a, **k):
            self.calls.append((name, a, k))
        return f


def interleave(P, recs):
    n = max(len(r.calls) for r in recs)
    for i in range(n):
        for r in recs:
            if i < len(r.calls):
                c = r.calls[i]
                getattr(P, c[0])(*c[1], **c[2])

_CP = {}
_o = 0
for _n, _w in [('cond', 8), ('flag', 1), ('n1', 32), ('n2', 32), ('fn', 8), ('adab', 192), ('s5d', 8),
               ('glub', 8), ('convw', 24), ('convb', 8), ('invf', 1), ('sign', 1), ('pcol', 2),
               ('lamre', 64), ('lamim', 64), ('lstep', 64), ('h0re', 64), ('h0im', 64), ('gam', 32),
               ('tloc', 256), ('dmat', 128), ('ip1', 128), ('imr', 128), ('eps', 1), ('zero', 1)]:
    _CP[_n] = (_o, _w)
    _o += _w
NCP = _o


def build(depth=4, dbg=False, mode='full'):
    import os
    NHEADS = int(os.environ.get('RET_HEADS', '8'))
    RSTOP = int(os.environ.get('RET_STOP', '99'))
    SKIP_HYB = os.environ.get('SKIP_HYB') == '1'
    SKIP_MLP = os.environ.get('SKIP_MLP') == '1'
    nc = bass.Bass("TRN2", target_bir_lowering=False)

    def din(name, shape):
        if mode == 'ret' and name in ('ada_w', 'hy_in_w', 'hy_out_w', 's5_glu_w', 'mlp_w1', 'mlp_w2'):
            shape = [1, 128, 128]
        return nc.dram_tensor(name, list(shape), F32, kind="ExternalInput").ap()

    def dout(name, shape):
        return nc.dram_tensor(name, list(shape), F32, kind="ExternalOutput").ap()

    xT_d = din("xT", [1024, TOK])
    cp_d = din("cpack", [128, NCP])
    cm_d = din("cmat", [128, 384])
    pos_d = din("pos", [128, TOK])
    s5p_d = din("s5pack", [128, 2, 4, 32, 16])
    s0_d = din("s0ret", [2, 2, 8, 128, 256])
    gnw_d = din("gnw", [2, 128, 2048])
    adaw_d = din("ada_w", [4, 1024, 6144])
    hyin_d = din("hy_in_w", [2, 1024, 2048])
    hyout_d = din("hy_out_w", [2, 1024, 1024])
    gluw_d = din("s5_glu_w", [2, 512, 512])
    retin_d = din("ret_in_w", [2, 1024, 6144])
    retout_d = din("ret_out_w", [2, 2048, 1024])
    w1_d = din("mlp_w1", [4, 1024, 4096])
    w2_d = din("mlp_w2", [4, 4096, 1024])
    yT_d = dout("yT", [1024, TOK])
    s5fin_d = dout("s5fin", [128, 2 * 2 * 16 * 8 * 2])
    retfin_d = dout("retfin", [2, 2, 8, 8, 128, 256])

    st = ExitStack()
    ARENA_BYTES = 207 * 1024
    arena = st.enter_context(nc.sbuf_tensor("arena", [128, ARENA_BYTES // 4], F32))
    psum = st.enter_context(nc.psum_tensor("ps", [128, 4096], F32))
    P = Prog(nc)

    def A32(off, n):
        assert off % 4 == 0 and off + 4 * n <= ARENA_BYTES, (off, n)
        return arena[:, off // 4: off // 4 + n]

    def A16(off, n):
        assert off % 4 == 0 and n % 2 == 0 and off + 2 * n <= ARENA_BYTES, (off, n)
        return arena[:, off // 4: off // 4 + n // 2].bitcast(BF16)

    def PS(bank, lo=0, n=512):
        return psum[:, bank * 512 + lo: bank * 512 + lo + n]

    def PSB(bank, lo=0, n=1024):
        return psum[:, bank * 512: bank * 512 + 512].bitcast(BF16)[:, lo:lo + n]

    off = 0

    def _al(n):
        nonlocal off
        o = off
        off += (n + 63) // 64 * 64
        return o
    XT_OFF = _al(8 * TOK * 4)
    HT_OFF = _al(8 * TOK * 2)
    CP_OFF = _al(NCP * 4)
    CM_OFF = _al(384 * 2)
    MOD_OFF = _al(2 * 48 * 4)
    AMOD_OFF = _al(2 * 16 * 4)
    SC_OFF = _al(8 * 4)
    SCB_OFF = _al(8 * 2 * 2)
    FIN_OFF = _al(2 * 2 * 16 * 8 * 2 * 4)
    IDF_OFF = _al(128 * 4)
    ROT_OFF = _al(2 * TOK * 2)
    off = (off + 63) // 64 * 64
    SCR = off
    SCR_BYTES = ARENA_BYTES - SCR
    print("persistent bytes", SCR, "scratch", SCR_BYTES)

    xT = A32(XT_OFF, 8 * TOK).rearrange("p (c t) -> p c t", t=TOK)
    hT = A16(HT_OFF, 8 * TOK).rearrange("p (c t) -> p c t", t=TOK)
    cp = A32(CP_OFF, NCP)

    def C(name, lo=0, n=None):
        o, w = _CP[name]
        if n is None:
            n = w - lo
        return cp[:, o + lo: o + lo + n]

    cm = A16(CM_OFF, 384)
    ident = cm[:, 0:128]
    perm = cm[:, 128:256]
    ones = cm[:, 256:384]
    MODS = [A32(MOD_OFF + 192 * i, 48) for i in range(2)]
    AMODS = [A32(AMOD_OFF + 64 * i, 16) for i in range(2)]
    mod = MODS[0]
    amod = AMODS[0]
    scT = A32(SC_OFF, 8)
    scTb = A16(SCB_OFF, 16).rearrange("p (k two) -> p k two", two=2)
    s5fin = A32(FIN_OFF, 1024)
    identF = A32(IDF_OFF, 128)
    cosT = A16(ROT_OFF, TOK)
    sinT = A16(ROT_OFF + TOK * 2, TOK)
    flag = C('flag')

    P.dma('sp', cp, cp_d, 'c0')
    P.dma('pool', cm, cm_d, 'c1')
    P.dma('sp', identF, cm_d[:, 0:128], 'c0')
    xTd3 = xT_d.rearrange("(c p) t -> p c t", p=128)
    for c in range(8):
        P.dma('sp', xT[:, c, :], xTd3[:, c, :], 'x')
    P.act(scT, C('cond'), AF.Silu)
    P.memset('dve', scTb, 0.0)
    P.copy('dve', scTb[:, :, 0], scT)
    P.memset('pool', s5fin, 0.0)

    class Scr:
        def __init__(self):
            self.o = SCR

        def take(self, nbytes):
            o = self.o
            self.o += (nbytes + 63) // 64 * 64
            assert self.o <= ARENA_BYTES, ("scratch overflow", self.o - ARENA_BYTES)
            return o

    ADA_BANK = 3

    def mods_steps(i, slots):
        tmod, tamod = MODS[i % 2], AMODS[i % 2]
        adw = adaw_d[i].rearrange("(kc p) n -> p kc n", p=128)
        steps = []

        wbs = {}

        def ld(g):
            wb = A16(slots[g % 2], 2048).rearrange("p (k n) -> p k n", n=256)
            P.dma('pool', wb, adw[:, :, g * 256:(g + 1) * 256], 'ada%d' % (g % 2))
            wbs[g] = wb

        def grp(g):
            if g == 0:
                ld(0)
            if g + 1 < 24:
                ld(g + 1)
            wb = wbs[g]
            for cc in range(2):
                col = g * 2 + cc
                for kc in range(8):
                    P.mm(PS(ADA_BANK, col, 2), wb[:, kc, cc * 128:(cc + 1) * 128], scTb[:, kc, :],
                         start=(kc == 0), stop=(kc == 7))

        def fin():
            P.tt('dve', tmod, PS(ADA_BANK, 0, 48), C('adab', i * 48, 48), ALU.add)
            P.stt('dve', tamod[:, 0:8], tmod[:, 8:16], 1.0, C('n1', i * 8, 8), ALU.add, ALU.mult)
            P.stt('dve', tamod[:, 8:16], tmod[:, 32:40], 1.0, C('n2', i * 8, 8), ALU.add, ALU.mult)

        for g in range(24):
            steps.append(lambda g=g: grp(g))
        steps.append(fin)
        return steps

    def mods(i):
        s = Scr()
        slots = [s.take(8 * 256 * 4) for _ in range(2)]
        for st_ in mods_steps(i, slots):
            st_()

    def norm_mod(scale_ap, shift_ap, out_fn):
        s = Scr()
        sq = [A16(s.take(1024), 512) for _ in range(2)]
        rstd = A32(s.take(2048), 512)
        tmp = [A32(s.take(2048), 512) for _ in range(2)]
        for tb in range(4):
            tsl = slice(tb * 512, (tb + 1) * 512)
            for c in range(8):
                P.tt('pool', sq[c % 2], xT[:, c, tsl], xT[:, c, tsl], ALU.mult)
                P.mm(PS(tb % 2), ones, sq[c % 2], start=(c == 0), stop=(c == 7))
            P.act(rstd, PS(tb % 2), AF.Sqrt, bias=C('eps'), scale=1.0 / 1024.0)
            P.recip(rstd, rstd)
            for c in range(8):
                P.tt('dve', tmp[c % 2], xT[:, c, tsl], rstd, ALU.mult)
                if shift_ap is None:
                    P.act(out_fn(c, tb), tmp[c % 2], AF.Identity, scale=scale_ap[:, c:c + 1])
                else:
                    P.act(out_fn(c, tb), tmp[c % 2], AF.Identity, bias=shift_ap[:, c:c + 1],
                          scale=scale_ap[:, c:c + 1])

    def hT_out(c, tb):
        return hT[:, c, tb * 512:(tb + 1) * 512]

    def mlp(i, bg_layer=None):
        s = Scr()
        w1s = [s.take(8 * 512 * 2) for _ in range(2)]
        w2s = [s.take(4 * 1024 * 2) for _ in range(2)]
        hids = [s.take(4 * 512 * 2) for _ in range(2)]
        rls = [s.take(512 * 2) for _ in range(2)]
        bg = []
        if bg_layer is not None:
            aslots = [s.take(8 * 256 * 4) for _ in range(2)]
            bg = mods_steps(bg_layer, aslots)
        w1d = w1_d[i].rearrange("(kc p) n -> p kc n", p=128)
        w2d = w2_d[i].rearrange("(kc p) n -> p kc n", p=128)
        g2 = mod[:, 40:48]
        wtiles = {}

        def load_w(hg):
            w1 = A16(w1s[hg % 2], 4096).rearrange("p (k n) -> p k n", n=512)
            w2 = A16(w2s[hg % 2], 4096).rearrange("p (k n) -> p k n", n=1024)
            P.dma('pool', w1, w1d[:, :, hg * 512:(hg + 1) * 512], 'w1_%d' % (hg % 2))
            P.dma('pool', w2, w2d[:, hg * 4:(hg + 1) * 4, :], 'w2_%d' % (hg % 2))
            wtiles[hg] = (w1, w2)

        def hid_phase(n):
            hg, tb = divmod(n, 4)
            w1 = wtiles[hg][0]
            tsl = slice(tb * 512, (tb + 1) * 512)
            hid = A16(hids[n % 2], 2048).rearrange("p (m n) -> p m n", n=512)
            for m in range(4):
                hb = (n * 4 + m) % 3
                for kc in range(8):
                    P.mm(PS(hb), w1[:, kc, m * 128:(m + 1) * 128], hT[:, kc, tsl], start=(kc == 0), stop=(kc == 7))
                rl = A16(rls[m % 2], 512)
                P.act(rl, PS(hb), AF.Relu)
                P.tt('pool', hid[:, m, :], rl, rl, ALU.mult)

        def out_phase(n):
            hg, tb = divmod(n, 4)
            w2 = wtiles[hg][1]
            tsl = slice(tb * 512, (tb + 1) * 512)
            hid = A16(hids[n % 2], 2048).rearrange("p (m n) -> p m n", n=512)
            for dc in range(8):
                pb = 4 + dc % 4
                for m in range(4):
                    P.mm(PS(pb), w2[:, m, dc * 128:(dc + 1) * 128], hid[:, m, :], start=(m == 0), stop=(m == 3))
                P.stt('dve', xT[:, dc, tsl], PS(pb), g2[:, dc:dc + 1], xT[:, dc, tsl], ALU.mult, ALU.add)

        load_w(0)
        load_w(1)
        hid_phase(0)
        for n in range(32):
            if n + 1 < 32:
                hid_phase(n + 1)
            out_phase(n)
            if n % 4 == 3 and n // 4 + 2 < 8:
                load_w(n // 4 + 2)
            if bg:
                bg.pop(0)()
        while bg:
            bg.pop(0)()

    def sincos(eng, y, n, sin_out, cos_out, t1, t2, P=P):
        P.ts(eng, t1, y, MAGIC, ALU.add, MAGIC, ALU.subtract)
        P.tt(eng, y, y, t1, ALU.subtract)
        P.act(sin_out, y, AF.Sin, scale=TWO_PI)
        P.ts(eng, t2, y, 0.25, ALU.add)
        P.ts(eng, t1, t2, MAGIC, ALU.add, MAGIC, ALU.subtract)
        P.tt(eng, t2, t2, t1, ALU.subtract)
        P.act(cos_out, t2, AF.Sin, scale=TWO_PI)

    def hybrid(j):
        g1 = mod[:, 16:24]
        s = Scr()
        wsl = [s.take(8 * 512 * 2) for _ in range(2)]
        MIXC_OFF = s.take(4 * TOK * 2)
        UT_OFF = s.take(4 * TOK * 2)
        uT = A16(UT_OFF, 4 * TOK).rearrange("p (c t) -> p c t", t=TOK)
        BIG0 = s.take(TOK * 4)
        REGA = s.take(16384)
        REGB = s.take(8192)
        BIG1 = REGB

        def mix(kc):
            if kc < 4:
                return A16(REGA + kc * 4096, TOK)
            return A16(MIXC_OFF + (kc - 4) * 4096, TOK)

        S5P_OFF = s.take(2 * 32 * 16 * 4)
        PRM_OFF = s.take(13 * 32 * 4)
        BB_OFF = s.take(2 * 32 * 16 * 4)
        s5b = A32(REGA, 1024).rearrange("p (a s c) -> p a s c", a=2, c=16)
        s5c = A32(S5P_OFF, 1024).rearrange("p (a s c) -> p a s c", a=2, c=16)
        prm = A32(PRM_OFF, 13 * 32).rearrange("p (a s) -> p a s", s=32)
        bb = A32(BB_OFF, 1024).rearrange("p (a s c) -> p a s c", a=2, c=16)
        BbT = [A16(s.take(2 * 128 * 2), 256).rearrange("p (a n) -> p a n", n=128) for _ in range(2)]
        CB = [A16(s.take(2 * 128 * 2), 256).rearrange("p (a n) -> p a n", n=128) for _ in range(2)]
        tiny = A32(s.take(16 * 4), 16)
        print('hybrid scratch used', s.o - SCR, 'of', SCR_BYTES)
        cbase = [REGA, wsl[0]]
        assert wsl[1] == wsl[0] + 8192
        V2 = lambda o: A32(o, 512).rearrange("p (a n) -> p a n", n=256)
        busb = [[V2(cbase[d]), V2(cbase[d] + 12288)] for d in range(2)]
        xb2 = [A16(cbase[d] + 14336, 512).rearrange("p (a n) -> p a n", n=256) for d in range(2)]
        rtP = [V2(cbase[d] + 2048) for d in range(2)]
        bp = [[V2(cbase[d] + 4096 + 2048 * i) for i in range(2)] for d in range(2)]
        wv = [V2(cbase[d] + 8192) for d in range(2)]
        rtP2 = [[V2(cbase[d] + 2048), V2(cbase[d] + 10240)] for d in range(2)]
        rtPF = [[A32(cbase[d] + 2048, 512), A32(cbase[d] + 10240, 512)] for d in range(2)]
        bpF = [[A32(cbase[d] + 4096 + 2048 * i, 512) for i in range(2)] for d in range(2)]
        rtD = [V2(cbase[d] + 10240) for d in range(2)]
        rb = REGB
        Etab = [A32(rb + 3072 * i, 768).rearrange("p (a n) -> p a n", n=256) for i in range(2)]
        xb = [A16(rb + 6144 + 1024 * i, 512).rearrange("p (a n) -> p a n", n=256) for i in range(2)]
        ytmp = [A32(BIG0 + 3072 * i, 256) for i in range(2)]
        ttmp = [A32(BIG0 + 3072 * i + 1024, 512).rearrange("p (a n) -> p a n", n=256) for i in range(2)]
        BBs = A32(BIG0 + 6144, 256).rearrange("p (a n) -> p a n", n=128)
        BBsb = A16(BIG0 + 7168, 256).rearrange("p (a n) -> p a n", n=128)
        sgt = [A32(BIG0 + 2048 * i, 512) for i in range(2)]

        P.dma('sp', s5b, s5p_d[:, j, 0:2], 'c0')
        P.dma('sp', s5c, s5p_d[:, j, 2:4], 'c0')
        lamre = C('lamre', j * 32, 32)
        lamim = C('lamim', j * 32, 32)
        lstep = C('lstep', j * 32, 32)
        DT, RR, THN, COS, SIN, COSF, SINF, FRE, FIM, T1, T2, T3, DEN = range(13)
        pa = lambda k: prm[:, k, :]
        P.act(pa(DT), lstep, AF.Exp)
        P.tt('dve', pa(T1), lamre, pa(DT), ALU.mult)
        P.act(pa(RR), pa(T1), AF.Exp)
        P.tt('dve', pa(T1), lamim, pa(DT), ALU.mult)
        P.ts('dve', pa(THN), pa(T1), 1.0 / TWO_PI, ALU.mult)
        P.copy('dve', pa(T1), pa(THN))
        sincos('dve', pa(T1), 32, pa(SIN), pa(COS), pa(T2), pa(T3))
        P.ts('dve', pa(COSF), pa(COS), flag, ALU.mult)
        P.ts('dve', pa(SINF), pa(SIN), flag, ALU.mult)
        P.tt('dve', pa(T1), pa(RR), pa(COS), ALU.mult)
        P.ts('dve', pa(T1), pa(T1), -1.0, ALU.add)
        P.tt('dve', pa(T2), pa(RR), pa(SIN), ALU.mult)
        P.tt('dve', pa(DEN), lamre, lamre, ALU.mult)
        P.tt('dve', pa(T3), lamim, lamim, ALU.mult)
        P.tt('dve', pa(DEN), pa(DEN), pa(T3), ALU.add)
        P.recip(pa(DEN), pa(DEN))
        P.tt('dve', pa(FRE), pa(T1), lamre, ALU.mult)
        P.tt('dve', pa(T3), pa(T2), lamim, ALU.mult)
        P.tt('dve', pa(FRE), pa(FRE), pa(T3), ALU.add)
        P.tt('dve', pa(FRE), pa(FRE), pa(DEN), ALU.mult)
        P.tt('dve', pa(FIM), pa(T2), lamre, ALU.mult)
        P.tt('dve', pa(T3), pa(T1), lamim, ALU.mult)
        P.tt('dve', pa(FIM), pa(FIM), pa(T3), ALU.subtract)
        P.tt('dve', pa(FIM), pa(FIM), pa(DEN), ALU.mult)
        fre_b = pa(FRE).unsqueeze(2).to_broadcast([128, 32, 16])
        fim_b = pa(FIM).unsqueeze(2).to_broadcast([128, 32, 16])
        tmp3 = A32(BIG0, 512).rearrange("p (s c) -> p s c", c=16)
        P.tt('dve', bb[:, 0], s5b[:, 0], fre_b, ALU.mult)
        P.tt('dve', tmp3, s5b[:, 1], fim_b, ALU.mult)
        P.tt('dve', bb[:, 0], bb[:, 0], tmp3, ALU.subtract)
        P.tt('dve', bb[:, 1], s5b[:, 1], fre_b, ALU.mult)
        P.tt('dve', tmp3, s5b[:, 0], fim_b, ALU.mult)
        P.tt('dve', bb[:, 1], bb[:, 1], tmp3, ALU.add)
        NSIN, NSINF = T1, T2
        P.ts('dve', pa(NSIN), pa(SIN), -1.0, ALU.mult)
        P.ts('dve', pa(NSINF), pa(SINF), -1.0, ALU.mult)

        hyin = hyin_d[j].rearrange("(kc p) n -> p kc n", p=128)
        w = A16(wsl[0], 4096).rearrange("p (k n) -> p k n", n=512)
        P.dma('pool', w, hyin[:, :, 0:512], 'hw0')
        for m in range(4):
            for tb in range(4):
                tsl = slice(tb * 512, (tb + 1) * 512)
                pb = (m * 4 + tb) % 4
                for kc in range(8):
                    P.mm(PS(pb), w[:, kc, m * 128:(m + 1) * 128], hT[:, kc, tsl], start=(kc == 0), stop=(kc == 7))
                P.copy('act', uT[:, m, tsl], PS(pb))

        cgv = A32(BIG0, TOK)
        acc = A32(BIG1, TOK)
        cgv3 = cgv.rearrange("p (s t) -> p s t", t=SEG)
        acc3 = acc.rearrange("p (s t) -> p s t", t=SEG)
        wf = A32(s.take(8 * 4), 8)
        wcs = {}

        def load_wc(m):
            wc_ = A16(wsl[(m + 1) % 2], 8 * 384).rearrange("p (k a n) -> p k a n", a=3, n=128)
            for a in range(3):
                P.dma('pool', wc_[:, :, a, :], hyin[:, :, 512 * (a + 1) + 128 * m: 512 * (a + 1) + 128 * (m + 1)],
                      'hw%d' % ((m + 1) % 2))
            wcs[m] = wc_

        load_wc(0)
        for m in range(4):
            if m + 1 < 4:
                load_wc(m + 1)
            wc = wcs[m]
            cw = C('convw', (j * 4 + m) * 3, 3)
            cb = C('convb', j * 4 + m, 1)
            for tb in range(4):
                tsl = slice(tb * 512, (tb + 1) * 512)
                for kc in range(8):
                    P.mm(PS(0 + tb % 2), wc[:, kc, 1, :], hT[:, kc, tsl], start=(kc == 0), stop=(kc == 7))
                for kc in range(8):
                    P.mm(PS(2 + tb % 2), wc[:, kc, 2, :], hT[:, kc, tsl], start=(kc == 0), stop=(kc == 7))
                P.copy('act', acc[:, tsl], PS(0 + tb % 2))
                P.tt('dve', cgv[:, tsl], acc[:, tsl], PS(2 + tb % 2), ALU.mult)
            P.ts('dve', wf[:, 0:1], cw[:, 0:1], flag, ALU.mult)
            P.ts('dve', wf[:, 1:2], cw[:, 2:3], flag, ALU.mult)
            P.act(acc, cgv, AF.Identity, bias=cb, scale=cw[:, 1:2])
            P.stt('dve', acc3[:, :, 1:], cgv3[:, :, :SEG - 1], cw[:, 0:1], acc3[:, :, 1:], ALU.mult, ALU.add)
            P.stt('dve', acc3[:, :, :SEG - 1], cgv3[:, :, 1:], cw[:, 2:3], acc3[:, :, :SEG - 1], ALU.mult, ALU.add)
            P.stt('dve', acc3[:, 1:, 0], cgv3[:, :NSEG - 1, SEG - 1], wf[:, 0:1], acc3[:, 1:, 0], ALU.mult, ALU.add)
            P.stt('dve', acc3[:, :NSEG - 1, SEG - 1], cgv3[:, 1:, 0], wf[:, 1:2], acc3[:, :NSEG - 1, SEG - 1],
                  ALU.mult, ALU.add)
            for tb in range(4):
                tsl = slice(tb * 512, (tb + 1) * 512)
                for kc in range(8):
                    P.mm(PS(4 + tb % 2), wc[:, kc, 0, :], hT[:, kc, tsl], start=(kc == 0), stop=(kc == 7))
                P.tt('dve', mix(4 + m)[:, tsl], acc[:, tsl], PS(4 + tb % 2), ALU.mult)

        ysb = A32(BIG0, TOK)

        def s5_setup(Q, q, sl, d):
            sidx = 4 * q + sl
            col = d * 16 + sidx
            pcol = lambda k_: prm[:, k_, col:col + 1]
            Q.memset('pool', BBs, 0.0)
            for gi in range(2):
                pr = slice(gi * 64, (gi + 1) * 64)
                c0 = 16 * (2 * sl + gi)
                for a in range(2):
                    Q.copy('pool', BBs[pr, a, c0:c0 + 16], bb[pr, a, col, :])
            Q.copy('pool', BBsb, BBs)
            for a in range(2):
                Q.transpose(PSB(2 + d, a * 128, 128), BBsb[:, a, :], ident)
            Q.copy('act', BbT[d], PSB(2 + d, 0, 256).rearrange("p (a n) -> p a n", n=128))
            Q.memset('pool', CB[d], 0.0)
            for gi in range(2):
                pr = slice(gi * 64, (gi + 1) * 64)
                c0 = 16 * (2 * sl + gi)
                Q.copy('pool', CB[d][pr, 0, c0:c0 + 16], s5c[pr, 0, col, :])
                Q.ts('pool', CB[d][pr, 1, c0:c0 + 16], s5c[pr, 1, col, :], -1.0, ALU.mult)
            Q.ts('dve', ytmp[d], C('tloc'), pcol(THN), ALU.mult)
            sincos('dve', ytmp[d], 256, Etab[d][:, 1, :], Etab[d][:, 0, :], ttmp[d][:, 0, :], ttmp[d][:, 1, :], P=Q)
            Q.ts('pool', Etab[d][:, 2, :], Etab[d][:, 1, :], -1.0, ALU.mult)

        def s5_segments(Q, q, sl, d):
            sidx = 4 * q + sl
            col = d * 16 + sidx
            pcol = lambda k_: prm[:, k_, col:col + 1]
            Ec = Etab[d][:, 0, :]
            Es = Etab[d][:, 1, :]
            EsN = Etab[d][:, 2, :]
            Ecb = Etab[d][:, 0, :].unsqueeze(1).to_broadcast([128, 2, SEG])
            rbc = pcol(RR).to_broadcast([128, SEG])
            segs = list(range(NSEG)) if d == 0 else list(range(NSEG - 1, -1, -1))
            fbase = ((j * 2 + d) * 16 + sidx) * 8

            def stageA1(si):
                seg = segs[si]
                tsl = slice(seg * SEG, (seg + 1) * SEG)
                pbu = d
                Q.mm(PS(pbu, 0, 256), BbT[d][:, 0, :], uT[:, q, tsl])
                Q.mm(PS(pbu, 256, 256), BbT[d][:, 1, :], uT[:, q, tsl])
                Q.copy('act', busb[d][si % 2], PS(pbu).rearrange("p (a n) -> p a n", n=256))

            def stageA2(si):
                bpk = bp[d][si % 2]
                bs = busb[d][si % 2]
                if d == 0:
                    b_nat, b_swp = bs, bs[:, ::-1, :]
                else:
                    b_nat, b_swp = bs[:, :, ::-1], bs[:, ::-1, ::-1]
                Q.tt('dve', rtP2[d][si % 2], Etab[d][:, 1:3, :], b_swp, ALU.mult)
                Q.tt('dve', bpk, Ecb, b_nat, ALU.mult)

            stageA1(0)
            stageA1(1)
            stageA2(0)
            for si, seg in enumerate(segs):
                if si + 2 < NSEG:
                    stageA1(si + 2)
                if si + 1 < NSEG:
                    stageA2(si + 1)
                Q.mm(PS(2 + d), identF, bpF[d][si % 2], start=True, stop=False)
                Q.mm(PS(2 + d), identF, rtPF[d][si % 2], start=False, stop=True)
                if si == 0:
                    xpr = C('h0re', j * 32 + col, 1)
                    xpi = C('h0im', j * 32 + col, 1)
                    cs, sn, nsn = pcol(COS), pcol(SIN), pcol(NSIN)
                else:
                    pv = fbase + segs[si - 1]
                    xpr = s5fin[:, 2 * pv:2 * pv + 1]
                    xpi = s5fin[:, 2 * pv + 1:2 * pv + 2]
                    cs, sn, nsn = pcol(COSF), pcol(SINF), pcol(NSINF)
                wi = tiny[:, 8 * d:8 * d + 8]
                Q.act(wi[:, 2:3], xpi, AF.Identity, scale=nsn)
                Q.act(wi[:, 0:1], xpr, AF.Identity, scale=cs, bias=wi[:, 2:3])
                Q.act(wi[:, 3:4], xpr, AF.Identity, scale=sn)
                Q.act(wi[:, 1:2], xpi, AF.Identity, scale=cs, bias=wi[:, 3:4])
                SCAN_ENG = os.environ.get('SCAN_ENG', 'dve')
                Q.scan(SCAN_ENG, wv[d][:, 0, :], rbc, PS(2 + d, 0, 256), wi[:, 0:1])
                Q.scan(SCAN_ENG, wv[d][:, 1, :], rbc, PS(2 + d, 256, 256), wi[:, 1:2])
                cur = fbase + seg
                fr_ = s5fin[:, 2 * cur:2 * cur + 1]
                fi_ = s5fin[:, 2 * cur + 1:2 * cur + 2]
                wre, wie = wv[d][:, 0, SEG - 1:SEG], wv[d][:, 1, SEG - 1:SEG]
                ece, ese = Ec[:, SEG - 1:SEG], Es[:, SEG - 1:SEG]
                esn = EsN[:, SEG - 1:SEG]
                Q.act(wi[:, 4:5], wie, AF.Identity, scale=esn)
                Q.act(fr_, wre, AF.Identity, scale=ece, bias=wi[:, 4:5])
                Q.act(wi[:, 5:6], wre, AF.Identity, scale=ese)
                Q.act(fi_, wie, AF.Identity, scale=ece, bias=wi[:, 5:6])
                x1 = xb[d] if d == 0 else xb[d][:, :, ::-1]
                x2 = xb2[d] if d == 0 else xb2[d][:, :, ::-1]
                Q.tt('dve', x1, Ecb, wv[d], ALU.mult)
                Q.tt('dve', x2, Etab[d][:, 2:0:-1, :], wv[d][:, ::-1, :], ALU.mult)
                yps = PS(4 + seg // 2, (seg % 2) * 256, 256)
                Q.mm(yps, CB[d][:, 0, :], xb[d][:, 0, :], start=False, stop=False, sgc=True)
                Q.mm(yps, CB[d][:, 1, :], xb[d][:, 1, :], start=False, stop=False, sgc=True)
                Q.mm(yps, CB[d][:, 0, :], xb2[d][:, 0, :], start=False, stop=False, sgc=True)
                Q.mm(yps, CB[d][:, 1, :], xb2[d][:, 1, :], start=False, stop=False, sgc=True)

        for q in range(4):
            for b4 in range(4):
                P.memset('dve', PS(4 + b4), 0.0)
            for sl in range(4):
                recs = [Rec(), Rec()]
                for d in range(2):
                    s5_setup(P, q, sl, d)
                for d in range(2):
                    s5_segments(recs[d], q, sl, d)
                interleave(P, recs)
            dcol = C('s5d', j * 4 + q, 1)
            for b4 in range(4):
                tsl = slice(b4 * 512, (b4 + 1) * 512)
                P.stt('dve', ysb[:, tsl], uT[:, q, tsl], dcol, PS(4 + b4), ALU.mult, ALU.add)
                P.act(uT[:, q, tsl], ysb[:, tsl], AF.Gelu_apprx_tanh)
        zT = uT
        gw = A16(wsl[0], 4 * 512).rearrange("p (k n) -> p k n", n=512)
        P.dma('pool', gw, gluw_d[j].rearrange("(kc p) n -> p kc n", p=128), 'hw0')
        for m in range(4):
            for tb in range(4):
                tsl = slice(tb * 512, (tb + 1) * 512)
                pb = (m * 4 + tb) % 4
                for kc in range(4):
                    P.mm(PS(pb), gw[:, kc, m * 128:(m + 1) * 128], zT[:, kc, tsl], start=(kc == 0), stop=(kc == 3))
                sg = sgt[tb % 2]
                P.act(sg, PS(pb), AF.Sigmoid, bias=C('glub', j * 4 + m, 1))
                P.tt('dve', mix(m)[:, tsl], zT[:, m, tsl], sg, ALU.mult)
        hyout = hyout_d[j].rearrange("(kc p) n -> p kc n", p=128)
        for dg in range(2):
            wo = A16(wsl[(dg + 1) % 2], 4096).rearrange("p (k n) -> p k n", n=512)
            P.dma('pool', wo, hyout[:, :, dg * 512:(dg + 1) * 512], 'hw%d' % ((dg + 1) % 2))
            for dl in range(4):
                dc = dg * 4 + dl
                for tb in range(4):
                    tsl = slice(tb * 512, (tb + 1) * 512)
                    pb = (dl * 4 + tb) % 4
                    for kc in range(8):
                        P.mm(PS(pb), wo[:, kc, dl * 128:(dl + 1) * 128], mix(kc)[:, tsl],
                             start=(kc == 0), stop=(kc == 7))
                    P.stt('dve', xT[:, dc, tsl], PS(pb), g1[:, dc:dc + 1], xT[:, dc, tsl], ALU.mult, ALU.add)

    def rotary_tables():
        s = Scr()
        y = A32(s.take(TOK * 4), TOK)
        t1 = A32(s.take(TOK * 4), TOK)
        t2 = A32(s.take(TOK * 4), TOK)
        sn = A32(s.take(TOK * 4), TOK)
        P.dma('sp', y, pos_d, 'c0')
        P.ts('dve', y, y, C('invf'), ALU.mult)
        sincos('dve', y, TOK, sn, cosT, t1, t2)
        P.ts('dve', sinT, sn, C('sign'), ALU.mult)

    def retention(j):
        nonlocal P
        g1 = mod[:, 16:24]
        s = Scr()
        WQK_OFF = s.take(8 * 256 * 2)
        WVG_OFF = s.take(8 * 512 * 2)
        GOT_OFF = s.take(2 * TOK * 2)
        WO_OFF = s.take(2 * 1024 * 2)
        Wqk = A16(WQK_OFF, 8 * 256).rearrange("p (k a n) -> p k a n", a=2, n=128)
        Wvg = A16(WVG_OFF, 8 * 512).rearrange("p (k a n) -> p k a n", a=2, n=256)
        Wvg2 = A16(WVG_OFF, 8 * 512).rearrange("p (k n) -> p k n", n=512)
        Wo = A16(WO_OFF, 2048).rearrange("p (k n) -> p k n", n=1024)
        goT = A16(GOT_OFF, 2 * TOK).rearrange("p (a t) -> p a t", t=TOK)
        qT = A16(s.take(TOK * 2), TOK)
        kT = A16(s.take(TOK * 2), TOK)
        QF = [A16(s.take(128 * 2), 128) for _ in range(2)]
        QB = [A16(s.take(128 * 2), 128) for _ in range(2)]
        kdf = A16(s.take(TOK * 2), TOK).rearrange("p (c d) -> p c d", d=128)
        kdb = A16(s.take(TOK * 2), TOK).rearrange("p (c d) -> p c d", d=128)
        vtm = A16(s.take(16 * 256 * 2), 4096).rearrange("p (c e) -> p c e", e=256)
        sgm = A16(s.take(16 * 256 * 2), 4096).rearrange("p (c e) -> p c e", e=256)
        SBst = A16(s.take(16 * 256 * 2), 4096).rearrange("p (c e) -> p c e", e=256)
        gnw = A16(s.take(2048 * 2), 2048)
        Mst = A32(s.take(4 * 256 * 4), 1024).rearrange("p (a e) -> p a e", e=256)
        maskT = A32(s.take(128 * 4), 128)
        mt = A32(s.take(2 * 128 * 4), 256).rearrange("p (a n) -> p a n", n=128)
        qd = A32(s.take(2 * 128 * 4), 256).rearrange("p (a n) -> p a n", n=128)
        lg = A32(s.take(16 * 4), 16)
        cd = A32(s.take(16 * 4), 16)
        cdf = A32(s.take(16 * 4), 16)
        kdc = A32(s.take(16 * 4), 16)
        cst = A32(s.take(4 * 128 * 4), 512).rearrange("p (a n) -> p a n", n=128)
        qraw = [A16(s.take(512 * 2), 512) for _ in range(2)]
        ssq = A32(s.take(16 * 4), 16)
        print('qraw off', [int(q.offset) for q in qraw], 'perm', int(perm.offset), 'ones', int(ones.offset))
        RR_ = s.take(8192)
        r1 = [A32(RR_ + 2048 * i, 512) for i in range(2)]
        r2 = [A32(RR_ + 4096 + 2048 * i, 512) for i in range(2)]
        osb = [A32(RR_ + 1024 * i, 256) for i in range(2)]
        onb = [A32(RR_ + 2048 + 1024 * i, 256) for i in range(2)]
        gob = [A16(RR_ + 4096 + 512 * i, 256) for i in range(2)]
        junk = A32(RR_ + 5120, 256)
        PT = [A16(RR_ + 6144 + 256 * i, 128) for i in range(2)]
        SFb = [A16(RR_ + 6656 + 512 * i, 256) for i in range(2)]
        print("retention scratch used", s.o - SCR)

        P.dma('pool', gnw, gnw_d[j], 'rw1')
        gam = C('gam', j * 16, 16)
        P.act(lg, gam, AF.Exp, scale=-1.0)
        P.ts('dve', lg, lg, 1.0, ALU.add)
        P.act(lg, lg, AF.Ln)
        P.ts('dve', lg, lg, -1.0, ALU.mult)
        P.act(cd, lg, AF.Exp, scale=128.0)
        P.ts('dve', cdf, cd, flag, ALU.mult)
        dm = C('dmat')
        P.ts('dve', cst[:, 0, :], dm, 0.0, ALU.max)
        P.ts('dve', cst[:, 1, :], dm, -1.0, ALU.mult, 0.0, ALU.max)
        P.ts('dve', cst[:, 2, :], dm, 0.0, ALU.is_ge)
        P.ts('dve', cst[:, 3, :], dm, 0.0, ALU.is_le)
        retin = retin_d[j].rearrange("(kc p) n -> p kc n", p=128)
        retout = retout_d[j].rearrange("(kc p) n -> p kc n", p=128)
        SCQ = 128.0 ** -0.5

        retqk = retin[:, :, 0:2048].rearrange("p k (a m) -> p k a m", a=2)
        retvg = retin[:, :, 2048:6144].rearrange("p k (a m) -> p k a m", a=2)

        def load_qk(hh):
            for a in range(2):
                P.dma('pool', Wqk[:, :, a, :], retqk[:, :, a, hh * 128:(hh + 1) * 128], 'rw0')

        def load_vg(hh):
            for a in range(2):
                P.dma('pool', Wvg[:, :, a, :], retvg[:, :, a, hh * 256:(hh + 1) * 256], 'rw2')

        def load_wo(hh):
            P.dma('pool', Wo, retout[:, 2 * hh:2 * hh + 2, :], 'rw1')

        def phaseA(h):
            lgf = lg[:, h:h + 1]
            lgb = lg[:, 8 + h:9 + h]
            if h == 0:
                load_qk(0)
                load_vg(0)
                load_wo(0)
            P.act(mt[:, 0, :], cst[:, 0, :], AF.Exp, scale=lgf)
            P.act(mt[:, 1, :], cst[:, 1, :], AF.Exp, scale=lgb)
            P.tt('dve', mt[:, 0, :], mt[:, 0, :], cst[:, 2, :], ALU.mult)
            P.tt('dve', mt[:, 1, :], mt[:, 1, :], cst[:, 3, :], ALU.mult)
            P.tt('dve', maskT, mt[:, 0, :], mt[:, 1, :], ALU.add)
            P.act(qd[:, 0, :], C('ip1'), AF.Exp, scale=lgf)
            P.act(qd[:, 1, :], C('imr'), AF.Exp, scale=lgb)
            P.act(kdc[:, 0:1], C('pcol', 0, 1), AF.Exp, scale=lgf)
            P.act(kdc[:, 1:2], C('pcol', 1, 1), AF.Exp, scale=lgb)
            qk = [(which, dst, scl, tb) for which, dst, scl in ((0, qT, SCQ), (1, kT, 1.0)) for tb in range(4)]

            def qkA(n):
                which, dst, scl, tb = qk[n]
                tsl = slice(tb * 512, (tb + 1) * 512)
                k2 = n % 2
                for kc in range(8):
                    P.mm(PS(k2), Wqk[:, kc, which, :], hT[:, kc, tsl],
                         start=(kc == 0), stop=(kc == 7))
                P.copy('act', qraw[k2], PS(k2))

            def qkB(n):
                which, dst, scl, tb = qk[n]
                tsl = slice(tb * 512, (tb + 1) * 512)
                k2 = n % 2
                P.mm(PS(2 + k2), perm, qraw[k2])
                P.stt('dve', r1[k2], PS(k2), scl, cosT[:, tsl], ALU.mult, ALU.mult)
                P.stt('dve', r2[k2], PS(2 + k2), scl, sinT[:, tsl], ALU.mult, ALU.mult)
                P.tt('pool', dst[:, tsl], r1[k2], r2[k2], ALU.add)

            qkA(0)
            for n in range(8):
                if n + 1 < 8:
                    qkA(n + 1)
                qkB(n)
            if h + 1 < NHEADS:
                load_qk(h + 1)
            for g4 in range(4):
                pb = 4 + g4 % 2
                for t4 in range(4):
                    c = g4 * 4 + t4
                    P.transpose(PSB(pb, t4 * 128, 128), kT[:, c * 128:(c + 1) * 128], ident)
                src = PSB(pb, 0, 512).rearrange("p (c d) -> p c d", d=128)
                P.act(kdf[:, g4 * 4:(g4 + 1) * 4, :], src, AF.Identity, scale=kdc[:, 0:1])
                P.act(kdb[:, g4 * 4:(g4 + 1) * 4, :], src, AF.Identity, scale=kdc[:, 1:2])
            for c in range(16):
                pb = 6 + c % 2
                for kc in range(8):
                    P.mm(PS(pb), hT[:, kc, c * 128:(c + 1) * 128], Wvg2[:, kc, :], start=(kc == 0), stop=(kc == 7))
                P.copy('act', vtm[:, c, :], PS(pb, 0, 256))
                P.act(sgm[:, c, :], PS(pb, 256, 256), AF.Silu)
            if h + 1 < NHEADS:
                load_vg(h + 1)
        def sweeps(h):
            MbP = [Mst[:, 2, :], Mst[:, 3, :]]
            MfP = [Mst[:, 0, :], Mst[:, 1, :]]
            P.dma('sp', MbP[1], s0_d[j, 1, h], 'rs')
            P.dma('sp', MfP[0], s0_d[j, 0, h], 'rs')
            for c in range(15, -1, -1):
                cross = (c != 15) and ((c + 1) % 2 == 0)
                Mb, Mbn = MbP[c % 2], MbP[(c + 1) % 2]
                if cross:
                    P.act(SBst[:, c, :], Mb, AF.Identity, scale=flag)
                else:
                    P.copy('act', SBst[:, c, :], Mb)
                pb = 4 + c % 2
                P.mm(PS(pb, 0, 256), kdb[:, c, :], vtm[:, c, :])
                dcol = (cdf if cross else cd)[:, 8 + h:9 + h]
                P.stt('dve', Mbn, Mb, dcol, PS(pb, 0, 256), ALU.mult, ALU.add)
                if c % 2 == 0:
                    P.dma('sp', retfin_d[j, 1, h, c // 2], Mbn, 'ro', is_out=True)
            def S1(c):
                k2 = c % 2
                csl = slice(c * 128, (c + 1) * 128)
                cross = (c != 0) and (c % 2 == 0)
                Mf, Mfn = MfP[c % 2], MfP[(c + 1) % 2]
                if cross:
                    P.act(SFb[k2], Mf, AF.Identity, scale=flag)
                else:
                    P.copy('act', SFb[k2], Mf)
                P.mm(PS(k2, 0, 128), kT[:, csl], qT[:, csl])
                P.mm(PS(6 + k2, 0, 256), kdf[:, c, :], vtm[:, c, :])
                P.tt('dve', QF[k2], qT[:, csl], qd[:, 0, :], ALU.mult)
                P.tt('dve', QB[k2], qT[:, csl], qd[:, 1, :], ALU.mult)
                P.tt('dve', PT[k2], PS(k2, 0, 128), maskT, ALU.mult)
                dcol = (cdf if cross else cd)[:, h:h + 1]
                P.stt('dve', Mfn, Mf, dcol, PS(6 + k2, 0, 256), ALU.mult, ALU.add)
                if c % 2 == 1:
                    P.dma('sp', retfin_d[j, 0, h, c // 2], Mfn, 'ro', is_out=True)

            def S2(c):
                k2 = c % 2
                csl = slice(c * 128, (c + 1) * 128)
                po = PS(2 + k2, 0, 256)
                P.mm(po, PT[k2], vtm[:, c, :], start=True, stop=False)
                P.mm(po, QF[k2], SFb[k2], start=False, stop=False)
                P.mm(po, QB[k2], SBst[:, c, :], start=False, stop=True)
                P.copy('act', osb[k2], po)
                P.act(junk, po, AF.Square)
                P.rsum(ssq[:, k2:k2 + 1], junk)
                P.act(ssq[:, 2 + k2:3 + k2], ssq[:, k2:k2 + 1], AF.Sqrt, bias=C('eps'), scale=1.0 / 256.0)
                P.recip(ssq[:, 2 + k2:3 + k2], ssq[:, 2 + k2:3 + k2])
                P.stt('dve', onb[k2], osb[k2], ssq[:, 2 + k2:3 + k2], gnw[:, h * 256:(h + 1) * 256], ALU.mult, ALU.mult)
                P.tt('pool', gob[k2], onb[k2], sgm[:, c, :], ALU.mult)

            def S3(c):
                k2 = c % 2
                csl = slice(c * 128, (c + 1) * 128)
                for ec in range(2):
                    P.transpose(PSB(4 + k2, ec * 128, 128), gob[k2][:, ec * 128:(ec + 1) * 128], ident)
                P.copy('act', goT[:, :, csl], PSB(4 + k2, 0, 256).rearrange("p (a i) -> p a i", i=128))

            for t in range(18):
                if t < 16:
                    S1(t)
                if 1 <= t <= 16:
                    S2(t - 1)
                if t >= 2:
                    S3(t - 2)
        def outproj(h, obanks):
            for dc in range(8):
                for tb in range(4):
                    tsl = slice(tb * 512, (tb + 1) * 512)
                    pb = obanks[(dc * 4 + tb) % len(obanks)]
                    for ec in range(2):
                        P.mm(PS(pb), Wo[:, ec, dc * 128:(dc + 1) * 128], goT[:, ec, tsl], start=(ec == 0), stop=(ec == 1))
                    P.stt('dve', xT[:, dc, tsl], PS(pb), g1[:, dc:dc + 1], xT[:, dc, tsl], ALU.mult, ALU.add)


        phaseA(0)
        for h in range(NHEADS):
            sweeps(h)
            if h + 1 < NHEADS:
                realP = P
                recO = Rec()
                P = recO
                outproj(h, [4, 5])
                recA = Rec()
                P = recA
                phaseA(h + 1)
                P = realP
                interleave(P, [recO, recA])
                load_wo(h + 1)
            else:
                outproj(h, [0, 1, 2, 3])

    if mode == 'ret':
        for c in range(8):
            P.copy('pool', hT[:, c, :], xT[:, c, :])
        P.memset('dve', mod, 0.5)
        rotary_tables()
        retention(0)
        depth = 0
    if depth > 0:
        mods(0)
    for i in range(depth):
        mod = MODS[i % 2]
        amod = AMODS[i % 2]
        norm_mod(amod[:, 0:8], mod[:, 0:8], hT_out)
        if i % 2 == 0:
            if not SKIP_HYB:
                hybrid(i // 2)
        else:
            if i == 1:
                rotary_tables()
            retention(i // 2)
        norm_mod(amod[:, 8:16], mod[:, 24:32], hT_out)
        if not SKIP_MLP:
            mlp(i, bg_layer=(i + 1 if i + 1 < depth else None))
        elif i + 1 < depth:
            mods(i + 1)
    if mode != 'ret':
        norm_mod(C('fn'), None, lambda c, tb: xT[:, c, tb * 512:(tb + 1) * 512])
    yTd3 = yT_d.rearrange("(c p) t -> p c t", p=128)
    for c in range(8):
        P.dma('sp', yTd3[:, c, :], xT[:, c, :], 'yo', is_out=True)
    P.dma('sp', s5fin_d, s5fin, 'yo', is_out=True)
    P.emit(st)
    print("ops", P.stats)
    st.close()
    return nc


def _sm(a):
    a = np.asarray(a, np.float32)
    pre = a.shape[:-2]
    a = a.reshape(pre + (16, 2, 64))
    nd = len(pre)
    a = np.transpose(a, (nd + 1, nd + 2) + tuple(range(nd)) + (nd,))
    return np.ascontiguousarray(a.reshape((128,) + pre + (16,)))


def _host_inputs(inp):
    f = lambda k: np.asarray(inp[k], np.float32)
    x_prompt, x_sample = f('x_prompt'), f('x_sample')
    common = {}
    for k in ('ada_w', 'hy_in_w', 'hy_out_w', 's5_glu_w', 'ret_in_w', 'ret_out_w', 'mlp_w1', 'mlp_w2'):
        common[k] = np.ascontiguousarray(f(k))
    common['gnw'] = np.ascontiguousarray(np.broadcast_to(f('ret_gn_w')[:, None, :], (2, 128, 2048)))
    cmat = np.zeros((128, 384), np.float32)
    cmat[:, 0:128] = np.eye(128, dtype=np.float32)
    for m in range(128):
        cmat[m ^ 32, 128 + m] = 1.0
    cmat[:, 256:384] = 1.0
    common['cmat'] = cmat
    bre = _sm(np.moveaxis(f('s5_b_re'), -1, 0))
    bim = _sm(np.moveaxis(f('s5_b_im'), -1, 0))
    cre = _sm(np.moveaxis(f('s5_c_re'), 3, 0))
    cim = _sm(np.moveaxis(f('s5_c_im'), 3, 0))
    pk = np.stack([np.transpose(a, (0, 2, 3, 4, 1)) for a in (bre, bim, cre, cim)], axis=2)
    common['s5pack'] = np.ascontiguousarray(pk.reshape(128, 2, 4, 32, 16))

    def chunked(v):
        v = np.asarray(v, np.float32)
        pre = v.shape[:-1]
        v = v.reshape(pre + (v.shape[-1] // 128, 128))
        return np.moveaxis(v, -1, 0)

    p = np.arange(128)
    base = np.zeros((128, NCP), np.float32)

    def put(name, arr):
        o, w = _CP[name]
        base[:, o:o + w] = np.asarray(arr, np.float32).reshape(128, w)

    put('n1', chunked(f('norm1_w')))
    put('n2', chunked(f('norm2_w')))
    put('fn', chunked(f('final_norm_w')))
    put('adab', chunked(f('ada_b')))
    put('s5d', chunked(f('s5_d')))
    put('glub', chunked(f('s5_glu_b')))
    put('convw', np.transpose(chunked(f('conv_w')), (0, 1, 3, 2)))
    put('convb', chunked(f('conv_b')))
    inv_freq = np.power(np.float32(10000.0), -np.arange(32, dtype=np.float32) / np.float32(32)).astype(np.float32)
    put('invf', (inv_freq[p % 32] / np.float32(TWO_PI)).astype(np.float32))
    put('sign', np.where((p % 64) < 32, -1.0, 1.0))
    put('pcol', np.stack([127.0 - p, p.astype(np.float32)], 1))
    put('lamre', _sm(f('s5_lam_re')))
    put('lamim', _sm(f('s5_lam_im')))
    put('lstep', _sm(np.broadcast_to(f('s5_log_step')[..., None], (2, 2, 32, 64))))
    put('gam', np.broadcast_to(f('ret_gamma_logit').reshape(1, 32), (128, 32)))
    put('tloc', np.broadcast_to(np.arange(256, dtype=np.float32)[None], (128, 256)))
    put('dmat', (np.arange(128)[None, :] - np.arange(128)[:, None]).astype(np.float32))
    put('ip1', np.broadcast_to(np.arange(1, 129, dtype=np.float32)[None], (128, 128)))
    put('imr', np.broadcast_to((128.0 - np.arange(128, dtype=np.float32))[None], (128, 128)))
    put('eps', np.full((128, 1), EPS, np.float32))
    t = np.arange(TOK)
    row = (t // 64).astype(np.float32)
    colp = (t % 64).astype(np.float32)
    pos_s = np.where(((p % 128) < 64)[:, None], row[None], colp[None]).astype(np.float32)
    maps = []
    for core in range(8):
        cpk = base.copy()

        def putc(name, arr):
            o, w = _CP[name]
            cpk[:, o:o + w] = np.asarray(arr, np.float32).reshape(128, w)

        m = dict(common)
        if core < 4:
            b = core
            xs = x_sample[b]
            putc('cond', chunked(f('c')[b]))
            putc('flag', np.ones((128, 1)))
            putc('h0re', _sm(f('state_s5_re')[b]))
            putc('h0im', _sm(f('state_s5_im')[b]))
            m['s0ret'] = np.ascontiguousarray(f('state_ret')[b])
            m['pos'] = pos_s
        else:
            xs = x_prompt[(core - 4) * 8:(core - 3) * 8].reshape(TOK, 1024)
            putc('cond', chunked(f('c_ctx')))
            m['s0ret'] = np.zeros((2, 2, 8, 128, 256), np.float32)
            m['pos'] = np.zeros((128, TOK), np.float32)
        m['xT'] = np.ascontiguousarray(xs.T)
        m['cpack'] = cpk
        maps.append(m)
    return maps


_NC_CACHE = {}


def kernel(**inputs):
    depth = 4
    if depth not in _NC_CACHE:
        _NC_CACHE[depth] = build(depth)
    nc = _NC_CACHE[depth]
    maps = _host_inputs(inputs)
    res = run_bass_kernel_spmd(nc, maps, core_ids=list(range(8)))
    R = res.results
    y_sample = np.stack([np.ascontiguousarray(R[b]['yT'].T) for b in range(4)], 0)
    y_prompt = np.concatenate([np.ascontiguousarray(R[c]['yT'].T).reshape(8, 256, 1024) for c in range(4, 8)], 0)
    new_re = np.zeros((32, 2, 2, 32, 64), np.float32)
    new_im = np.zeros((32, 2, 2, 32, 64), np.float32)
    new_ret = np.zeros((32, 2, 2, 8, 128, 256), np.float32)
    for c in range(4, 8):
        fin = R[c]['s5fin'].reshape(2, 64, 2, 2, 16, 8, 2)
        a = np.transpose(fin, (5, 2, 3, 4, 0, 1, 6)).reshape(8, 2, 2, 32, 64, 2)
        new_re[(c - 4) * 8:(c - 3) * 8] = a[..., 0]
        new_im[(c - 4) * 8:(c - 3) * 8] = a[..., 1]
        rf = R[c]['retfin']
        new_ret[(c - 4) * 8:(c - 3) * 8] = np.transpose(rf, (3, 0, 1, 2, 4, 5))
    return (y_prompt.astype(np.float32), y_sample.astype(np.float32), new_re, new_im, new_ret)
```

```python
import math
from contextlib import ExitStack
import numpy as np
import concourse.bass as bass
import concourse.mybir as mybir
from concourse.bass_utils import run_bass_kernel_spmd

F32 = mybir.dt.float32
BF16 = mybir.dt.bfloat16
ALU = mybir.AluOpType
AF = mybir.ActivationFunctionType

TOK = 2048
NSEG = 8
SEG = 256
EPS = 1e-6
MAGIC = 12582912.0
TWO_PI = 2.0 * math.pi


def _esz(dt):
    return 2 if dt == BF16 else 4


class Prog:
    def __init__(self, nc, self_sync=True):
        self.nc = nc
        self.ops = []
        self.hist = {}
        self.self_sync = self_sync
        self.out_groups = set()

    def _region(self, ap):
        t = ap.tensor
        tn = type(t).__name__
        if 'DRam' in tn:
            return None
        key = 'PSUM' if 'PSum' in tn else 'SBUF:' + t.name
        pairs = ap.ap
        off = int(ap.offset)
        pstep, pcnt = pairs[0]
        es = _esz(ap.dtype)
        if pstep == 0:
            p0 = 0
            rem = off
        else:
            p0 = off // pstep
            rem = off - p0 * pstep
        lo = 0
        hi = 0
        for st, cnt in pairs[1:]:
            d = st * (cnt - 1)
            if d < 0:
                lo += d
            else:
                hi += d
        b0 = (rem + lo) * es
        b1 = (rem + hi + 1) * es
        if key == 'PSUM':
            return key, 0, 128, b0 // 2048 * 2048, (b1 + 2047) // 2048 * 2048
        return key, p0, p0 + pcnt, b0, b1

    BUCKET = 2048

    def _access(self, ap, opid, is_w, eng, is_dma, mutate=True):
        r = self._region(ap)
        deps = set()
        if r is None:
            return deps
        key, p0, p1, b0, b1 = r
        hk = self.hist.setdefault(key, {})
        ent = (p0, p1, b0, b1, opid, is_w, eng, is_dma)
        for bk in range(b0 // self.BUCKET, (b1 - 1) // self.BUCKET + 1):
            h = hk.get(bk)
            if h is None:
                if mutate:
                    hk[bk] = [ent]
                continue
            keep = []
            for e in h:
                ep0, ep1, eb0, eb1, eid, ew, eeng, edma = e
                if eid == opid:
                    keep.append(e)
                    continue
                ov = not (ep1 <= p0 or p1 <= ep0 or eb1 <= b0 or b1 <= eb0)
                if ov and (is_w or ew or (key == 'PSUM' and eeng != eng)):
                    if not (is_w and ew and eng == 'pe' and eeng == 'pe'):
                        deps.add(eid)
                if not mutate:
                    continue
                contained = ep0 >= p0 and ep1 <= p1 and eb0 >= b0 and eb1 <= b1
                if is_w and contained:
                    continue
                if (not is_w) and (not ew) and contained and eeng == eng and not edma and not is_dma:
                    continue
                keep.append(e)
            if mutate:
                keep.append(ent)
                hk[bk] = keep
        return deps

    def add(self, eng, fn, reads=(), writes=(), dma=None):
        opid = len(self.ops)
        deps = set()
        isd = dma is not None
        for ap in reads:
            deps |= self._access(ap, opid, False, eng, isd, mutate=False)
        for ap in writes:
            deps |= self._access(ap, opid, True, eng, isd, mutate=False)
        for ap in reads:
            self._access(ap, opid, False, eng, isd, mutate=True)
        for ap in writes:
            self._access(ap, opid, True, eng, isd, mutate=True)
        self.ops.append(dict(eng=eng, fn=fn, deps=deps, dma=dma))
        return opid

    def _e(self, eng):
        nc = self.nc
        return {'pe': nc.tensor, 'act': nc.scalar, 'dve': nc.vector, 'pool': nc.gpsimd, 'sp': nc.sync}[eng]

    def mm(self, out, lhsT, rhs, start=True, stop=True, sgc=False):
        if sgc:
            self.add('pe', lambda: self.nc.tensor.matmul(out, lhsT, rhs, start=start, stop=stop,
                                                         skip_group_check=True),
                     reads=[lhsT, rhs], writes=[out])
        else:
            self.add('pe', lambda: self.nc.tensor.matmul(out, lhsT, rhs, start=start, stop=stop),
                     reads=[lhsT, rhs], writes=[out])

    def rsum(self, out, in_):
        self.add('dve', lambda: self.nc.vector.reduce_sum(out, in_, axis=mybir.AxisListType.X),
                 reads=[in_], writes=[out])

    def transpose(self, out, in_, ident):
        self.add('pe', lambda: self.nc.tensor.transpose(out, in_, ident), reads=[in_, ident], writes=[out])

    def act(self, out, in_, func, bias=None, scale=None, accum_out=None):
        kw = {}
        rd = [in_]
        wr = [out]
        if bias is not None:
            kw['bias'] = bias
            if not isinstance(bias, (int, float)):
                rd.append(bias)
        if scale is not None:
            kw['scale'] = scale
            if not isinstance(scale, (int, float)):
                rd.append(scale)
        if accum_out is not None:
            kw['accum_out'] = accum_out
            wr.append(accum_out)
        self.add('act', lambda: self.nc.scalar.activation(out, in_, func, **kw), reads=rd, writes=wr)

    def tt(self, eng, out, in0, in1, op):
        self.add(eng, lambda: self._e(eng).tensor_tensor(out, in0, in1, op), reads=[in0, in1], writes=[out])

    def ts(self, eng, out, in0, s1, op0, s2=None, op1=None):
        rd = [in0] + [s for s in (s1, s2) if s is not None and not isinstance(s, (int, float))]
        kw = {}
        if op1 is not None:
            kw['op1'] = op1
        self.add(eng, lambda: self._e(eng).tensor_scalar(out, in0, s1, s2, op0, **kw), reads=rd, writes=[out])

    def stt(self, eng, out, in0, scalar, in1, op0, op1):
        rd = [in0, in1] + ([scalar] if not isinstance(scalar, (int, float)) else [])
        self.add(eng, lambda: self._e(eng).scalar_tensor_tensor(out, in0, scalar, in1, op0, op1),
                 reads=rd, writes=[out])

    def scan(self, eng, out, d0, d1, init):
        rd = [d0, d1] + ([init] if not isinstance(init, (int, float)) else [])
        self.add(eng, lambda: self._e(eng).tensor_tensor_scan(out, d0, d1, init, ALU.mult, ALU.add),
                 reads=rd, writes=[out])

    def copy(self, eng, out, in_):
        if eng == 'act':
            return self.act(out, in_, AF.Copy)
        self.add(eng, lambda: self._e(eng).tensor_copy(out, in_), reads=[in_], writes=[out])

    def recip(self, out, in_):
        self.add('dve', lambda: self.nc.vector.reciprocal(out, in_), reads=[in_], writes=[out])

    def memset(self, eng, out, val):
        self.add(eng, lambda: self._e(eng).memset(out, val), reads=[], writes=[out])

    def dma(self, q, out, in_, group, is_out=False):
        if is_out:
            self.out_groups.add(group)
        self.add(q, lambda: self._e(q).dma_start(out, in_), reads=[in_], writes=[out], dma=group)

    def emit(self, stack):
        nc = self.nc
        ops = self.ops
        needed = set()
        for o in ops:
            needed |= o['deps']
        engs = ['pe', 'act', 'dve', 'pool', 'sp']
        sems = {e: stack.enter_context(nc.semaphore('s_' + e)) for e in engs}
        groups = sorted({o['dma'] for o in ops if o['dma'] is not None})
        gsem = {g: stack.enter_context(nc.semaphore('g_' + g)) for g in groups}
        cnt = {e: 0 for e in engs}
        gcnt = {g: 0 for g in groups}
        gcnt_at = []
        sig = {}
        for i, o in enumerate(ops):
            gcnt_at.append(dict(gcnt))
            if o['dma'] is not None:
                gcnt[o['dma']] += 16
                sig[i] = ('g', o['dma'], gcnt[o['dma']])
            elif i in needed:
                cnt[o['eng']] += 1
                sig[i] = ('e', o['eng'], cnt[o['eng']])
        per_eng = {e: [] for e in engs}
        for i, o in enumerate(ops):
            per_eng[o['eng']].append(i)
        self.stats = {e: len(per_eng[e]) for e in engs}
        self.stats.update({'sig_' + e: cnt[e] for e in engs})
        final_g = dict(gcnt)
        block = stack.enter_context(nc.Block())

        def run(e, handle):
            waited = {}
            nw = 0
            for i in per_eng[e]:
                o = ops[i]
                want = {}
                for d in o['deps']:
                    kind, who, val = sig[d]
                    if kind == 'g':
                        val = gcnt_at[i][who]
                        k = ('g', who)
                    else:
                        if who == e and not self.self_sync and o['dma'] is None:
                            continue
                        k = ('e', who)
                    if val > want.get(k, 0):
                        want[k] = val
                for k, val in want.items():
                    if waited.get(k, 0) >= val:
                        continue
                    waited[k] = val
                    s = gsem[k[1]] if k[0] == 'g' else sems[k[1]]
                    handle.wait_ge(s, val)
                    nw += 1
                ins = o['fn']()
                if i in sig:
                    kind, who, val = sig[i]
                    if kind == 'g':
                        ins.then_inc(gsem[who], 16)
                    else:
                        ins.then_inc(sems[who], 1)
            if e == 'sp':
                for g in sorted(self.out_groups):
                    handle.wait_ge(gsem[g], final_g[g])
            self.stats['w_' + e] = nw

        @block.tensor
        def _(h):
            run('pe', h)

        @block.scalar
        def _(h):
            run('act', h)

        @block.vector
        def _(h):
            run('dve', h)

        @block.gpsimd
        def _(h):
            run('pool', h)

        @block.sync
        def _(h):
            run('sp', h)


class Rec:
    def __init__(self):
        self.calls = []

    def __getattr__(self, name):
        def f(*a, **k):
            self.calls.append((name, a, k))
        return f


def interleave(P, recs):
    n = max(len(r.calls) for r in recs)
    for i in range(n):
        for r in recs:
            if i < len(r.calls):
                c = r.calls[i]
                getattr(P, c[0])(*c[1], **c[2])

_CP = {}
_o = 0
for _n, _w in [('cond', 8), ('flag', 1), ('n1', 32), ('n2', 32), ('fn', 8), ('adab', 192), ('s5d', 8),
               ('glub', 8), ('convw', 24), ('convb', 8), ('invf', 1), ('sign', 1), ('pcol', 2),
               ('lamre', 64), ('lamim', 64), ('lstep', 64), ('h0re', 64), ('h0im', 64), ('gam', 32),
               ('tloc', 256), ('dmat', 128), ('ip1', 128), ('imr', 128), ('eps', 1), ('zero', 1)]:
    _CP[_n] = (_o, _w)
    _o += _w
NCP = _o


def build(depth=4, dbg=False, mode='full'):
    import os
    NHEADS = int(os.environ.get('RET_HEADS', '8'))
    RSTOP = int(os.environ.get('RET_STOP', '99'))
    SKIP_HYB = os.environ.get('SKIP_HYB') == '1'
    SKIP_MLP = os.environ.get('SKIP_MLP') == '1'
    nc = bass.Bass("TRN2", target_bir_lowering=False)

    def din(name, shape):
        if mode == 'ret' and name in ('ada_w', 'hy_in_w', 'hy_out_w', 's5_glu_w', 'mlp_w1', 'mlp_w2'):
            shape = [1, 128, 128]
        return nc.dram_tensor(name, list(shape), F32, kind="ExternalInput").ap()

    def dout(name, shape):
        return nc.dram_tensor(name, list(shape), F32, kind="ExternalOutput").ap()

    xT_d = din("xT", [1024, TOK])
    cp_d = din("cpack", [128, NCP])
    cm_d = din("cmat", [128, 384])
    pos_d = din("pos", [128, TOK])
    s5p_d = din("s5pack", [128, 2, 4, 32, 16])
    s0_d = din("s0ret", [2, 2, 8, 128, 256])
    gnw_d = din("gnw", [2, 128, 2048])
    adaw_d = din("ada_w", [4, 1024, 6144])
    hyin_d = din("hy_in_w", [2, 1024, 2048])
    hyout_d = din("hy_out_w", [2, 1024, 1024])
    gluw_d = din("s5_glu_w", [2, 512, 512])
    retin_d = din("ret_in_w", [2, 1024, 6144])
    retout_d = din("ret_out_w", [2, 2048, 1024])
    w1_d = din("mlp_w1", [4, 1024, 4096])
    w2_d = din("mlp_w2", [4, 4096, 1024])
    yT_d = dout("yT", [1024, TOK])
    s5fin_d = dout("s5fin", [128, 2 * 2 * 16 * 8 * 2])
    retfin_d = dout("retfin", [2, 2, 8, 8, 128, 256])

    st = ExitStack()
    ARENA_BYTES = 207 * 1024
    arena = st.enter_context(nc.sbuf_tensor("arena", [128, ARENA_BYTES // 4], F32))
    psum = st.enter_context(nc.psum_tensor("ps", [128, 4096], F32))
    P = Prog(nc)

    def A32(off, n):
        assert off % 4 == 0 and off + 4 * n <= ARENA_BYTES, (off, n)
        return arena[:, off // 4: off // 4 + n]

    def A16(off, n):
        assert off % 4 == 0 and n % 2 == 0 and off + 2 * n <= ARENA_BYTES, (off, n)
        return arena[:, off // 4: off // 4 + n // 2].bitcast(BF16)

    def PS(bank, lo=0, n=512):
        return psum[:, bank * 512 + lo: bank * 512 + lo + n]

    def PSB(bank, lo=0, n=1024):
        return psum[:, bank * 512: bank * 512 + 512].bitcast(BF16)[:, lo:lo + n]

    off = 0

    def _al(n):
        nonlocal off
        o = off
        off += (n + 63) // 64 * 64
        return o
    XT_OFF = _al(8 * TOK * 4)
    HT_OFF = _al(8 * TOK * 2)
    CP_OFF = _al(NCP * 4)
    CM_OFF = _al(384 * 2)
    MOD_OFF = _al(2 * 48 * 4)
    AMOD_OFF = _al(2 * 16 * 4)
    SC_OFF = _al(8 * 4)
    SCB_OFF = _al(8 * 2 * 2)
    FIN_OFF = _al(2 * 2 * 16 * 8 * 2 * 4)
    MISC_OFF = _al(64 * 4)
    ROT_OFF = _al(2 * TOK * 2)
    off = (off + 63) // 64 * 64
    SCR = off
    SCR_BYTES = ARENA_BYTES - SCR
    print("persistent bytes", SCR, "scratch", SCR_BYTES)

    xT = A32(XT_OFF, 8 * TOK).rearrange("p (c t) -> p c t", t=TOK)
    hT = A16(HT_OFF, 8 * TOK).rearrange("p (c t) -> p c t", t=TOK)
    cp = A32(CP_OFF, NCP)

    def C(name, lo=0, n=None):
        o, w = _CP[name]
        if n is None:
            n = w - lo
        return cp[:, o + lo: o + lo + n]

    cm = A16(CM_OFF, 384)
    ident = cm[:, 0:128]
    perm = cm[:, 128:256]
    ones = cm[:, 256:384]
    MODS = [A32(MOD_OFF + 192 * i, 48) for i in range(2)]
    AMODS = [A32(AMOD_OFF + 64 * i, 16) for i in range(2)]
    mod = MODS[0]
    amod = AMODS[0]
    scT = A32(SC_OFF, 8)
    scTb = A16(SCB_OFF, 16).rearrange("p (k two) -> p k two", two=2)
    s5fin = A32(FIN_OFF, 1024)
    misc = A32(MISC_OFF, 64)
    cosT = A16(ROT_OFF, TOK)
    sinT = A16(ROT_OFF + TOK * 2, TOK)
    flag = C('flag')

    P.dma('sp', cp, cp_d, 'c0')
    P.dma('pool', cm, cm_d, 'c1')
    xTd3 = xT_d.rearrange("(c p) t -> p c t", p=128)
    for c in range(8):
        P.dma('sp', xT[:, c, :], xTd3[:, c, :], 'x')
    P.act(scT, C('cond'), AF.Silu)
    P.memset('dve', scTb, 0.0)
    P.copy('dve', scTb[:, :, 0], scT)
    P.memset('pool', s5fin, 0.0)

    class Scr:
        def __init__(self):
            self.o = SCR

        def take(self, nbytes):
            o = self.o
            self.o += (nbytes + 63) // 64 * 64
            assert self.o <= ARENA_BYTES, ("scratch overflow", self.o - ARENA_BYTES)
            return o

    ADA_BANK = 3

    def mods_steps(i, slots):
        tmod, tamod = MODS[i % 2], AMODS[i % 2]
        adw = adaw_d[i].rearrange("(kc p) n -> p kc n", p=128)
        steps = []

        wbs = {}

        def ld(g):
            wb = A16(slots[g % 2], 2048).rearrange("p (k n) -> p k n", n=256)
            P.dma('pool', wb, adw[:, :, g * 256:(g + 1) * 256], 'ada%d' % (g % 2))
            wbs[g] = wb

        def grp(g):
            if g == 0:
                ld(0)
            if g + 1 < 24:
                ld(g + 1)
            wb = wbs[g]
            for cc in range(2):
                col = g * 2 + cc
                for kc in range(8):
                    P.mm(PS(ADA_BANK, col, 2), wb[:, kc, cc * 128:(cc + 1) * 128], scTb[:, kc, :],
                         start=(kc == 0), stop=(kc == 7))

        def fin():
            P.tt('dve', tmod, PS(ADA_BANK, 0, 48), C('adab', i * 48, 48), ALU.add)
            P.stt('dve', tamod[:, 0:8], tmod[:, 8:16], 1.0, C('n1', i * 8, 8), ALU.add, ALU.mult)
            P.stt('dve', tamod[:, 8:16], tmod[:, 32:40], 1.0, C('n2', i * 8, 8), ALU.add, ALU.mult)

        for g in range(24):
            steps.append(lambda g=g: grp(g))
        steps.append(fin)
        return steps

    def mods(i):
        s = Scr()
        slots = [s.take(8 * 256 * 4) for _ in range(2)]
        for st_ in mods_steps(i, slots):
            st_()

    def norm_mod(scale_ap, shift_ap, out_fn):
        s = Scr()
        sq = [A16(s.take(1024), 512) for _ in range(2)]
        rstd = A32(s.take(2048), 512)
        tmp = [A32(s.take(2048), 512) for _ in range(2)]
        for tb in range(4):
            tsl = slice(tb * 512, (tb + 1) * 512)
            for c in range(8):
                P.tt('pool', sq[c % 2], xT[:, c, tsl], xT[:, c, tsl], ALU.mult)
                P.mm(PS(tb % 2), ones, sq[c % 2], start=(c == 0), stop=(c == 7))
            P.act(rstd, PS(tb % 2), AF.Sqrt, bias=C('eps'), scale=1.0 / 1024.0)
            P.recip(rstd, rstd)
            for c in range(8):
                P.tt('dve', tmp[c % 2], xT[:, c, tsl], rstd, ALU.mult)
                if shift_ap is None:
                    P.act(out_fn(c, tb), tmp[c % 2], AF.Identity, scale=scale_ap[:, c:c + 1])
                else:
                    P.act(out_fn(c, tb), tmp[c % 2], AF.Identity, bias=shift_ap[:, c:c + 1],
                          scale=scale_ap[:, c:c + 1])

    def hT_out(c, tb):
        return hT[:, c, tb * 512:(tb + 1) * 512]

    def mlp(i, bg_layer=None):
        s = Scr()
        w1s = [s.take(8 * 512 * 2) for _ in range(2)]
        w2s = [s.take(4 * 1024 * 2) for _ in range(2)]
        hids = [s.take(4 * 512 * 2) for _ in range(2)]
        rls = [s.take(512 * 2) for _ in range(2)]
        bg = []
        if bg_layer is not None:
            aslots = [s.take(8 * 256 * 4) for _ in range(2)]
            bg = mods_steps(bg_layer, aslots)
        w1d = w1_d[i].rearrange("(kc p) n -> p kc n", p=128)
        w2d = w2_d[i].rearrange("(kc p) n -> p kc n", p=128)
        g2 = mod[:, 40:48]
        wtiles = {}

        def load_w(hg):
            w1 = A16(w1s[hg % 2], 4096).rearrange("p (k n) -> p k n", n=512)
            w2 = A16(w2s[hg % 2], 4096).rearrange("p (k n) -> p k n", n=1024)
            P.dma('pool', w1, w1d[:, :, hg * 512:(hg + 1) * 512], 'w1_%d' % (hg % 2))
            P.dma('pool', w2, w2d[:, hg * 4:(hg + 1) * 4, :], 'w2_%d' % (hg % 2))
            wtiles[hg] = (w1, w2)

        def hid_phase(n):
            hg, tb = divmod(n, 4)
            w1 = wtiles[hg][0]
            tsl = slice(tb * 512, (tb + 1) * 512)
            hid = A16(hids[n % 2], 2048).rearrange("p (m n) -> p m n", n=512)
            for m in range(4):
                hb = (n * 4 + m) % 3
                for kc in range(8):
                    P.mm(PS(hb), w1[:, kc, m * 128:(m + 1) * 128], hT[:, kc, tsl], start=(kc == 0), stop=(kc == 7))
                rl = A16(rls[m % 2], 512)
                P.act(rl, PS(hb), AF.Relu)
                P.tt('pool', hid[:, m, :], rl, rl, ALU.mult)

        def out_phase(n):
            hg, tb = divmod(n, 4)
            w2 = wtiles[hg][1]
            tsl = slice(tb * 512, (tb + 1) * 512)
            hid = A16(hids[n % 2], 2048).rearrange("p (m n) -> p m n", n=512)
            for dc in range(8):
                pb = 4 + dc % 4
                for m in range(4):
                    P.mm(PS(pb), w2[:, m, dc * 128:(dc + 1) * 128], hid[:, m, :], start=(m == 0), stop=(m == 3))
                P.stt('dve', xT[:, dc, tsl], PS(pb), g2[:, dc:dc + 1], xT[:, dc, tsl], ALU.mult, ALU.add)

        load_w(0)
        load_w(1)
        hid_phase(0)
        for n in range(32):
            if n + 1 < 32:
                hid_phase(n + 1)
            out_phase(n)
            if n % 4 == 3 and n // 4 + 2 < 8:
                load_w(n // 4 + 2)
            if bg:
                bg.pop(0)()
        while bg:
            bg.pop(0)()

    def sincos(eng, y, n, sin_out, cos_out, t1, t2, P=P):
        P.ts(eng, t1, y, MAGIC, ALU.add, MAGIC, ALU.subtract)
        P.tt(eng, y, y, t1, ALU.subtract)
        P.act(sin_out, y, AF.Sin, scale=TWO_PI)
        P.ts(eng, t2, y, 0.25, ALU.add)
        P.ts(eng, t1, t2, MAGIC, ALU.add, MAGIC, ALU.subtract)
        P.tt(eng, t2, t2, t1, ALU.subtract)
        P.act(cos_out, t2, AF.Sin, scale=TWO_PI)

    def hybrid(j):
        g1 = mod[:, 16:24]
        s = Scr()
        wsl = [s.take(8 * 512 * 2) for _ in range(2)]
        MIXC_OFF = s.take(4 * TOK * 2)
        UT_OFF = s.take(4 * TOK * 2)
        uT = A16(UT_OFF, 4 * TOK).rearrange("p (c t) -> p c t", t=TOK)
        BIG0 = s.take(TOK * 4)
        REGA = s.take(16384)
        REGB = s.take(8192)
        BIG1 = REGB

        def mix(kc):
            if kc < 4:
                return A16(REGA + kc * 4096, TOK)
            return A16(MIXC_OFF + (kc - 4) * 4096, TOK)

        S5P_OFF = s.take(2 * 32 * 16 * 4)
        PRM_OFF = s.take(13 * 32 * 4)
        BB_OFF = s.take(2 * 32 * 16 * 4)
        s5b = A32(REGA, 1024).rearrange("p (a s c) -> p a s c", a=2, c=16)
        s5c = A32(S5P_OFF, 1024).rearrange("p (a s c) -> p a s c", a=2, c=16)
        prm = A32(PRM_OFF, 13 * 32).rearrange("p (a s) -> p a s", s=32)
        bb = A32(BB_OFF, 1024).rearrange("p (a s c) -> p a s c", a=2, c=16)
        BbT = [A16(s.take(2 * 128 * 2), 256).rearrange("p (a n) -> p a n", n=128) for _ in range(2)]
        CB = [A16(s.take(2 * 128 * 2), 256).rearrange("p (a n) -> p a n", n=128) for _ in range(2)]
        tiny = A32(s.take(64 * 4), 64)
        cbase = [REGA, wsl[0]]
        assert wsl[1] == wsl[0] + 8192
        V2 = lambda o: A32(o, 512).rearrange("p (a n) -> p a n", n=256)
        busb = [[V2(cbase[d]), V2(cbase[d] + 12288)] for d in range(2)]
        xb2 = [A16(cbase[d] + 14336, 512).rearrange("p (a n) -> p a n", n=256) for d in range(2)]
        rtP = [V2(cbase[d] + 2048) for d in range(2)]
        bp = [[V2(cbase[d] + 4096 + 2048 * i) for i in range(2)] for d in range(2)]
        wv = [V2(cbase[d] + 8192) for d in range(2)]
        rtD = [V2(cbase[d] + 10240) for d in range(2)]
        rb = REGB
        Etab = [A32(rb + 3072 * i, 768).rearrange("p (a n) -> p a n", n=256) for i in range(2)]
        xb = [A16(rb + 6144 + 1024 * i, 512).rearrange("p (a n) -> p a n", n=256) for i in range(2)]
        ytmp = [A32(BIG0 + 3072 * i, 256) for i in range(2)]
        ttmp = [A32(BIG0 + 3072 * i + 1024, 512).rearrange("p (a n) -> p a n", n=256) for i in range(2)]
        BBs = A32(BIG0 + 6144, 256).rearrange("p (a n) -> p a n", n=128)
        BBsb = A16(BIG0 + 7168, 256).rearrange("p (a n) -> p a n", n=128)
        sgt = [A32(BIG0 + 2048 * i, 512) for i in range(2)]

        P.dma('sp', s5b, s5p_d[:, j, 0:2], 'c0')
        P.dma('sp', s5c, s5p_d[:, j, 2:4], 'c0')
        lamre = C('lamre', j * 32, 32)
        lamim = C('lamim', j * 32, 32)
        lstep = C('lstep', j * 32, 32)
        DT, RR, THN, COS, SIN, COSF, SINF, FRE, FIM, T1, T2, T3, DEN = range(13)
        pa = lambda k: prm[:, k, :]
        P.act(pa(DT), lstep, AF.Exp)
        P.tt('dve', pa(T1), lamre, pa(DT), ALU.mult)
        P.act(pa(RR), pa(T1), AF.Exp)
        P.tt('dve', pa(T1), lamim, pa(DT), ALU.mult)
        P.ts('dve', pa(THN), pa(T1), 1.0 / TWO_PI, ALU.mult)
        P.copy('dve', pa(T1), pa(THN))
        sincos('dve', pa(T1), 32, pa(SIN), pa(COS), pa(T2), pa(T3))
        P.ts('dve', pa(COSF), pa(COS), flag, ALU.mult)
        P.ts('dve', pa(SINF), pa(SIN), flag, ALU.mult)
        P.tt('dve', pa(T1), pa(RR), pa(COS), ALU.mult)
        P.ts('dve', pa(T1), pa(T1), -1.0, ALU.add)
        P.tt('dve', pa(T2), pa(RR), pa(SIN), ALU.mult)
        P.tt('dve', pa(DEN), lamre, lamre, ALU.mult)
        P.tt('dve', pa(T3), lamim, lamim, ALU.mult)
        P.tt('dve', pa(DEN), pa(DEN), pa(T3), ALU.add)
        P.recip(pa(DEN), pa(DEN))
        P.tt('dve', pa(FRE), pa(T1), lamre, ALU.mult)
        P.tt('dve', pa(T3), pa(T2), lamim, ALU.mult)
        P.tt('dve', pa(FRE), pa(FRE), pa(T3), ALU.add)
        P.tt('dve', pa(FRE), pa(FRE), pa(DEN), ALU.mult)
        P.tt('dve', pa(FIM), pa(T2), lamre, ALU.mult)
        P.tt('dve', pa(T3), pa(T1), lamim, ALU.mult)
        P.tt('dve', pa(FIM), pa(FIM), pa(T3), ALU.subtract)
        P.tt('dve', pa(FIM), pa(FIM), pa(DEN), ALU.mult)
        fre_b = pa(FRE).unsqueeze(2).to_broadcast([128, 32, 16])
        fim_b = pa(FIM).unsqueeze(2).to_broadcast([128, 32, 16])
        tmp3 = A32(BIG0, 512).rearrange("p (s c) -> p s c", c=16)
        P.tt('dve', bb[:, 0], s5b[:, 0], fre_b, ALU.mult)
        P.tt('dve', tmp3, s5b[:, 1], fim_b, ALU.mult)
        P.tt('dve', bb[:, 0], bb[:, 0], tmp3, ALU.subtract)
        P.tt('dve', bb[:, 1], s5b[:, 1], fre_b, ALU.mult)
        P.tt('dve', tmp3, s5b[:, 0], fim_b, ALU.mult)
        P.tt('dve', bb[:, 1], bb[:, 1], tmp3, ALU.add)
        NSIN, NSINF = T1, T2
        P.ts('dve', pa(NSIN), pa(SIN), -1.0, ALU.mult)
        P.ts('dve', pa(NSINF), pa(SINF), -1.0, ALU.mult)

        hyin = hyin_d[j].rearrange("(kc p) n -> p kc n", p=128)
        w = A16(wsl[0], 4096).rearrange("p (k n) -> p k n", n=512)
        P.dma('pool', w, hyin[:, :, 0:512], 'hw0')
        for m in range(4):
            for tb in range(4):
                tsl = slice(tb * 512, (tb + 1) * 512)
                pb = (m * 4 + tb) % 4
                for kc in range(8):
                    P.mm(PS(pb), w[:, kc, m * 128:(m + 1) * 128], hT[:, kc, tsl], start=(kc == 0), stop=(kc == 7))
                P.copy('act', uT[:, m, tsl], PS(pb))

        cgv = A32(BIG0, TOK)
        acc = A32(BIG1, TOK)
        cgv3 = cgv.rearrange("p (s t) -> p s t", t=SEG)
        acc3 = acc.rearrange("p (s t) -> p s t", t=SEG)
        wf = A32(s.take(8 * 4), 8)
        wcs = {}

        def load_wc(m):
            wc_ = A16(wsl[(m + 1) % 2], 8 * 384).rearrange("p (k a n) -> p k a n", a=3, n=128)
            for a in range(3):
                P.dma('pool', wc_[:, :, a, :], hyin[:, :, 512 * (a + 1) + 128 * m: 512 * (a + 1) + 128 * (m + 1)],
                      'hw%d' % ((m + 1) % 2))
            wcs[m] = wc_

        load_wc(0)
        for m in range(4):
            if m + 1 < 4:
                load_wc(m + 1)
            wc = wcs[m]
            cw = C('convw', (j * 4 + m) * 3, 3)
            cb = C('convb', j * 4 + m, 1)
            for tb in range(4):
                tsl = slice(tb * 512, (tb + 1) * 512)
                for kc in range(8):
                    P.mm(PS(0 + tb % 2), wc[:, kc, 1, :], hT[:, kc, tsl], start=(kc == 0), stop=(kc == 7))
                for kc in range(8):
                    P.mm(PS(2 + tb % 2), wc[:, kc, 2, :], hT[:, kc, tsl], start=(kc == 0), stop=(kc == 7))
                P.copy('act', acc[:, tsl], PS(0 + tb % 2))
                P.tt('dve', cgv[:, tsl], acc[:, tsl], PS(2 + tb % 2), ALU.mult)
            P.ts('dve', wf[:, 0:1], cw[:, 0:1], flag, ALU.mult)
            P.ts('dve', wf[:, 1:2], cw[:, 2:3], flag, ALU.mult)
            P.act(acc, cgv, AF.Identity, bias=cb, scale=cw[:, 1:2])
            P.stt('dve', acc3[:, :, 1:], cgv3[:, :, :SEG - 1], cw[:, 0:1], acc3[:, :, 1:], ALU.mult, ALU.add)
            P.stt('dve', acc3[:, :, :SEG - 1], cgv3[:, :, 1:], cw[:, 2:3], acc3[:, :, :SEG - 1], ALU.mult, ALU.add)
            P.stt('dve', acc3[:, 1:, 0], cgv3[:, :NSEG - 1, SEG - 1], wf[:, 0:1], acc3[:, 1:, 0], ALU.mult, ALU.add)
            P.stt('dve', acc3[:, :NSEG - 1, SEG - 1], cgv3[:, 1:, 0], wf[:, 1:2], acc3[:, :NSEG - 1, SEG - 1],
                  ALU.mult, ALU.add)
            for tb in range(4):
                tsl = slice(tb * 512, (tb + 1) * 512)
                for kc in range(8):
                    P.mm(PS(4 + tb % 2), wc[:, kc, 0, :], hT[:, kc, tsl], start=(kc == 0), stop=(kc == 7))
                P.tt('dve', mix(4 + m)[:, tsl], acc[:, tsl], PS(4 + tb % 2), ALU.mult)

        ysb = A32(BIG0, TOK)

        def s5_setup(Q, q, sl, d):
            sidx = 4 * q + sl
            col = d * 16 + sidx
            pcol = lambda k_: prm[:, k_, col:col + 1]
            Q.memset('pool', BBs, 0.0)
            for gi in range(2):
                pr = slice(gi * 64, (gi + 1) * 64)
                c0 = 16 * (2 * sl + gi)
                for a in range(2):
                    Q.copy('pool', BBs[pr, a, c0:c0 + 16], bb[pr, a, col, :])
            Q.copy('pool', BBsb, BBs)
            for a in range(2):
                Q.transpose(PSB(2 + d, a * 128, 128), BBsb[:, a, :], ident)
            Q.copy('act', BbT[d], PSB(2 + d, 0, 256).rearrange("p (a n) -> p a n", n=128))
            Q.memset('pool', CB[d], 0.0)
            for gi in range(2):
                pr = slice(gi * 64, (gi + 1) * 64)
                c0 = 16 * (2 * sl + gi)
                Q.copy('pool', CB[d][pr, 0, c0:c0 + 16], s5c[pr, 0, col, :])
                Q.ts('pool', CB[d][pr, 1, c0:c0 + 16], s5c[pr, 1, col, :], -1.0, ALU.mult)
            Q.ts('dve', ytmp[d], C('tloc'), pcol(THN), ALU.mult)
            sincos('dve', ytmp[d], 256, Etab[d][:, 1, :], Etab[d][:, 0, :], ttmp[d][:, 0, :], ttmp[d][:, 1, :], P=Q)
            Q.ts('pool', Etab[d][:, 2, :], Etab[d][:, 1, :], -1.0, ALU.mult)

        def s5_segments(Q, q, sl, d):
            sidx = 4 * q + sl
            col = d * 16 + sidx
            pcol = lambda k_: prm[:, k_, col:col + 1]
            Ec = Etab[d][:, 0, :]
            Es = Etab[d][:, 1, :]
            EsN = Etab[d][:, 2, :]
            Ecb = Etab[d][:, 0, :].unsqueeze(1).to_broadcast([128, 2, SEG])
            rbc = pcol(RR).to_broadcast([128, SEG])
            segs = list(range(NSEG)) if d == 0 else list(range(NSEG - 1, -1, -1))
            fbase = ((j * 2 + d) * 16 + sidx) * 8

            def stageA1(si):
                seg = segs[si]
                tsl = slice(seg * SEG, (seg + 1) * SEG)
                pbu = d
                Q.mm(PS(pbu, 0, 256), BbT[d][:, 0, :], uT[:, q, tsl])
                Q.mm(PS(pbu, 256, 256), BbT[d][:, 1, :], uT[:, q, tsl])
                Q.copy('act', busb[d][si % 2], PS(pbu).rearrange("p (a n) -> p a n", n=256))

            def stageA2(si):
                bpk = bp[d][si % 2]
                bs = busb[d][si % 2]
                if d == 0:
                    b_nat, b_swp = bs, bs[:, ::-1, :]
                else:
                    b_nat, b_swp = bs[:, :, ::-1], bs[:, ::-1, ::-1]
                Q.tt('dve', rtP[d], Etab[d][:, 1:3, :], b_swp, ALU.mult)
                Q.tt('dve', bpk, Ecb, b_nat, ALU.mult)
                Q.tt('dve', bpk, bpk, rtP[d], ALU.add)

            stageA1(0)
            stageA1(1)
            stageA2(0)
            for si, seg in enumerate(segs):
                if si + 2 < NSEG:
                    stageA1(si + 2)
                if si + 1 < NSEG:
                    stageA2(si + 1)
                bpk = bp[d][si % 2]
                if si == 0:
                    xpr = C('h0re', j * 32 + col, 1)
                    xpi = C('h0im', j * 32 + col, 1)
                    cs, sn, nsn = pcol(COS), pcol(SIN), pcol(NSIN)
                else:
                    pv = fbase + segs[si - 1]
                    xpr = s5fin[:, 2 * pv:2 * pv + 1]
                    xpi = s5fin[:, 2 * pv + 1:2 * pv + 2]
                    cs, sn, nsn = pcol(COSF), pcol(SINF), pcol(NSINF)
                wi = tiny[:, 8 * d:8 * d + 8]
                Q.act(wi[:, 2:3], xpi, AF.Identity, scale=nsn)
                Q.act(wi[:, 0:1], xpr, AF.Identity, scale=cs, bias=wi[:, 2:3])
                Q.act(wi[:, 3:4], xpr, AF.Identity, scale=sn)
                Q.act(wi[:, 1:2], xpi, AF.Identity, scale=cs, bias=wi[:, 3:4])
                SCAN_ENG = os.environ.get('SCAN_ENG', 'dve')
                Q.scan(SCAN_ENG, wv[d][:, 0, :], rbc, bpk[:, 0, :], wi[:, 0:1])
                Q.scan(SCAN_ENG, wv[d][:, 1, :], rbc, bpk[:, 1, :], wi[:, 1:2])
                cur = fbase + seg
                fr_ = s5fin[:, 2 * cur:2 * cur + 1]
                fi_ = s5fin[:, 2 * cur + 1:2 * cur + 2]
                wre, wie = wv[d][:, 0, SEG - 1:SEG], wv[d][:, 1, SEG - 1:SEG]
                ece, ese = Ec[:, SEG - 1:SEG], Es[:, SEG - 1:SEG]
                esn = EsN[:, SEG - 1:SEG]
                Q.act(wi[:, 4:5], wie, AF.Identity, scale=esn)
                Q.act(fr_, wre, AF.Identity, scale=ece, bias=wi[:, 4:5])
                Q.act(wi[:, 5:6], wre, AF.Identity, scale=ese)
                Q.act(fi_, wie, AF.Identity, scale=ece, bias=wi[:, 5:6])
                x1 = xb[d] if d == 0 else xb[d][:, :, ::-1]
                x2 = xb2[d] if d == 0 else xb2[d][:, :, ::-1]
                Q.tt('dve', x1, Ecb, wv[d], ALU.mult)
                Q.tt('dve', x2, Etab[d][:, 2:0:-1, :], wv[d][:, ::-1, :], ALU.mult)
                yps = PS(4 + seg // 2, (seg % 2) * 256, 256)
                Q.mm(yps, CB[d][:, 0, :], xb[d][:, 0, :], start=False, stop=False, sgc=True)
                Q.mm(yps, CB[d][:, 1, :], xb[d][:, 1, :], start=False, stop=False, sgc=True)
                Q.mm(yps, CB[d][:, 0, :], xb2[d][:, 0, :], start=False, stop=False, sgc=True)
                Q.mm(yps, CB[d][:, 1, :], xb2[d][:, 1, :], start=False, stop=False, sgc=True)

        for q in range(4):
            for b4 in range(4):
                P.memset('dve', PS(4 + b4), 0.0)
            for sl in range(4):
                recs = [Rec(), Rec()]
                for d in range(2):
                    s5_setup(P, q, sl, d)
                for d in range(2):
                    s5_segments(recs[d], q, sl, d)
                interleave(P, recs)
            dcol = C('s5d', j * 4 + q, 1)
            for b4 in range(4):
                tsl = slice(b4 * 512, (b4 + 1) * 512)
                P.stt('dve', ysb[:, tsl], uT[:, q, tsl], dcol, PS(4 + b4), ALU.mult, ALU.add)
                P.act(uT[:, q, tsl], ysb[:, tsl], AF.Gelu_apprx_tanh)
        zT = uT
        gw = A16(wsl[0], 4 * 512).rearrange("p (k n) -> p k n", n=512)
        P.dma('pool', gw, gluw_d[j].rearrange("(kc p) n -> p kc n", p=128), 'hw0')
        for m in range(4):
            for tb in range(4):
                tsl = slice(tb * 512, (tb + 1) * 512)
                pb = (m * 4 + tb) % 4
                for kc in range(4):
                    P.mm(PS(pb), gw[:, kc, m * 128:(m + 1) * 128], zT[:, kc, tsl], start=(kc == 0), stop=(kc == 3))
                sg = sgt[tb % 2]
                P.act(sg, PS(pb), AF.Sigmoid, bias=C('glub', j * 4 + m, 1))
                P.tt('dve', mix(m)[:, tsl], zT[:, m, tsl], sg, ALU.mult)
        hyout = hyout_d[j].rearrange("(kc p) n -> p kc n", p=128)
        for dg in range(2):
            wo = A16(wsl[(dg + 1) % 2], 4096).rearrange("p (k n) -> p k n", n=512)
            P.dma('pool', wo, hyout[:, :, dg * 512:(dg + 1) * 512], 'hw%d' % ((dg + 1) % 2))
            for dl in range(4):
                dc = dg * 4 + dl
                for tb in range(4):
                    tsl = slice(tb * 512, (tb + 1) * 512)
                    pb = (dl * 4 + tb) % 4
                    for kc in range(8):
                        P.mm(PS(pb), wo[:, kc, dl * 128:(dl + 1) * 128], mix(kc)[:, tsl],
                             start=(kc == 0), stop=(kc == 7))
                    P.stt('dve', xT[:, dc, tsl], PS(pb), g1[:, dc:dc + 1], xT[:, dc, tsl], ALU.mult, ALU.add)

    def rotary_tables():
        s = Scr()
        y = A32(s.take(TOK * 4), TOK)
        t1 = A32(s.take(TOK * 4), TOK)
        t2 = A32(s.take(TOK * 4), TOK)
        sn = A32(s.take(TOK * 4), TOK)
        P.dma('sp', y, pos_d, 'c0')
        P.ts('dve', y, y, C('invf'), ALU.mult)
        sincos('dve', y, TOK, sn, cosT, t1, t2)
        P.ts('dve', sinT, sn, C('sign'), ALU.mult)

    def retention(j):
        nonlocal P
        g1 = mod[:, 16:24]
        s = Scr()
        WQK_OFF = s.take(8 * 256 * 2)
        WVG_OFF = s.take(8 * 512 * 2)
        GOT_OFF = s.take(2 * TOK * 2)
        WO_OFF = s.take(2 * 1024 * 2)
        Wqk = A16(WQK_OFF, 8 * 256).rearrange("p (k a n) -> p k a n", a=2, n=128)
        Wvg = A16(WVG_OFF, 8 * 512).rearrange("p (k a n) -> p k a n", a=2, n=256)
        Wvg2 = A16(WVG_OFF, 8 * 512).rearrange("p (k n) -> p k n", n=512)
        Wo = A16(WO_OFF, 2048).rearrange("p (k n) -> p k n", n=1024)
        goT = A16(GOT_OFF, 2 * TOK).rearrange("p (a t) -> p a t", t=TOK)
        qT = A16(s.take(TOK * 2), TOK)
        kT = A16(s.take(TOK * 2), TOK)
        QF = [A16(s.take(128 * 2), 128) for _ in range(2)]
        QB = [A16(s.take(128 * 2), 128) for _ in range(2)]
        kdf = A16(s.take(TOK * 2), TOK).rearrange("p (c d) -> p c d", d=128)
        kdb = A16(s.take(TOK * 2), TOK).rearrange("p (c d) -> p c d", d=128)
        vtm = A16(s.take(16 * 256 * 2), 4096).rearrange("p (c e) -> p c e", e=256)
        sgm = A16(s.take(16 * 256 * 2), 4096).rearrange("p (c e) -> p c e", e=256)
        SBst = A16(s.take(16 * 256 * 2), 4096).rearrange("p (c e) -> p c e", e=256)
        gnw = A16(s.take(2048 * 2), 2048)
        Mst = A32(s.take(4 * 256 * 4), 1024).rearrange("p (a e) -> p a e", e=256)
        maskT = A32(s.take(128 * 4), 128)
        mt = A32(s.take(2 * 128 * 4), 256).rearrange("p (a n) -> p a n", n=128)
        qd = A32(s.take(2 * 128 * 4), 256).rearrange("p (a n) -> p a n", n=128)
        lg = A32(s.take(16 * 4), 16)
        cd = A32(s.take(16 * 4), 16)
        cdf = A32(s.take(16 * 4), 16)
        kdc = A32(s.take(16 * 4), 16)
        cst = A32(s.take(4 * 128 * 4), 512).rearrange("p (a n) -> p a n", n=128)
        qraw = [A16(s.take(512 * 2), 512) for _ in range(2)]
        ssq = A32(s.take(16 * 4), 16)
        print('qraw off', [int(q.offset) for q in qraw], 'perm', int(perm.offset), 'ones', int(ones.offset))
        RR_ = s.take(8192)
        r1 = [A32(RR_ + 2048 * i, 512) for i in range(2)]
        r2 = [A32(RR_ + 4096 + 2048 * i, 512) for i in range(2)]
        osb = [A32(RR_ + 1024 * i, 256) for i in range(2)]
        onb = [A32(RR_ + 2048 + 1024 * i, 256) for i in range(2)]
        gob = [A16(RR_ + 4096 + 512 * i, 256) for i in range(2)]
        junk = A32(RR_ + 5120, 256)
        PT = [A16(RR_ + 6144 + 256 * i, 128) for i in range(2)]
        SFb = [A16(RR_ + 6656 + 512 * i, 256) for i in range(2)]
        print("retention scratch used", s.o - SCR)

        P.dma('pool', gnw, gnw_d[j], 'rw1')
        gam = C('gam', j * 16, 16)
        P.act(lg, gam, AF.Exp, scale=-1.0)
        P.ts('dve', lg, lg, 1.0, ALU.add)
        P.act(lg, lg, AF.Ln)
        P.ts('dve', lg, lg, -1.0, ALU.mult)
        P.act(cd, lg, AF.Exp, scale=128.0)
        P.ts('dve', cdf, cd, flag, ALU.mult)
        dm = C('dmat')
        P.ts('dve', cst[:, 0, :], dm, 0.0, ALU.max)
        P.ts('dve', cst[:, 1, :], dm, -1.0, ALU.mult, 0.0, ALU.max)
        P.ts('dve', cst[:, 2, :], dm, 0.0, ALU.is_ge)
        P.ts('dve', cst[:, 3, :], dm, 0.0, ALU.is_le)
        retin = retin_d[j].rearrange("(kc p) n -> p kc n", p=128)
        retout = retout_d[j].rearrange("(kc p) n -> p kc n", p=128)
        SCQ = 128.0 ** -0.5

        retqk = retin[:, :, 0:2048].rearrange("p k (a m) -> p k a m", a=2)
        retvg = retin[:, :, 2048:6144].rearrange("p k (a m) -> p k a m", a=2)

        def load_qk(hh):
            for a in range(2):
                P.dma('pool', Wqk[:, :, a, :], retqk[:, :, a, hh * 128:(hh + 1) * 128], 'rw0')

        def load_vg(hh):
            for a in range(2):
                P.dma('pool', Wvg[:, :, a, :], retvg[:, :, a, hh * 256:(hh + 1) * 256], 'rw2')

        def load_wo(hh):
            P.dma('pool', Wo, retout[:, 2 * hh:2 * hh + 2, :], 'rw1')

        def phaseA(h):
            lgf = lg[:, h:h + 1]
            lgb = lg[:, 8 + h:9 + h]
            if h == 0:
                load_qk(0)
                load_vg(0)
                load_wo(0)
            P.act(mt[:, 0, :], cst[:, 0, :], AF.Exp, scale=lgf)
            P.act(mt[:, 1, :], cst[:, 1, :], AF.Exp, scale=lgb)
            P.tt('dve', mt[:, 0, :], mt[:, 0, :], cst[:, 2, :], ALU.mult)
            P.tt('dve', mt[:, 1, :], mt[:, 1, :], cst[:, 3, :], ALU.mult)
            P.tt('dve', maskT, mt[:, 0, :], mt[:, 1, :], ALU.add)
            P.act(qd[:, 0, :], C('ip1'), AF.Exp, scale=lgf)
            P.act(qd[:, 1, :], C('imr'), AF.Exp, scale=lgb)
            P.act(kdc[:, 0:1], C('pcol', 0, 1), AF.Exp, scale=lgf)
            P.act(kdc[:, 1:2], C('pcol', 1, 1), AF.Exp, scale=lgb)
            qk = [(which, dst, scl, tb) for which, dst, scl in ((0, qT, SCQ), (1, kT, 1.0)) for tb in range(4)]

            def qkA(n):
                which, dst, scl, tb = qk[n]
                tsl = slice(tb * 512, (tb + 1) * 512)
                k2 = n % 2
                for kc in range(8):
                    P.mm(PS(k2), Wqk[:, kc, which, :], hT[:, kc, tsl],
                         start=(kc == 0), stop=(kc == 7))
                P.copy('act', qraw[k2], PS(k2))

            def qkB(n):
                which, dst, scl, tb = qk[n]
                tsl = slice(tb * 512, (tb + 1) * 512)
                k2 = n % 2
                P.mm(PS(2 + k2), perm, qraw[k2])
                P.stt('dve', r1[k2], PS(k2), scl, cosT[:, tsl], ALU.mult, ALU.mult)
                P.stt('dve', r2[k2], PS(2 + k2), scl, sinT[:, tsl], ALU.mult, ALU.mult)
                P.tt('pool', dst[:, tsl], r1[k2], r2[k2], ALU.add)

            qkA(0)
            for n in range(8):
                if n + 1 < 8:
                    qkA(n + 1)
                qkB(n)
            if h + 1 < NHEADS:
                load_qk(h + 1)
            for g4 in range(4):
                pb = 4 + g4 % 2
                for t4 in range(4):
                    c = g4 * 4 + t4
                    P.transpose(PSB(pb, t4 * 128, 128), kT[:, c * 128:(c + 1) * 128], ident)
                src = PSB(pb, 0, 512).rearrange("p (c d) -> p c d", d=128)
                P.act(kdf[:, g4 * 4:(g4 + 1) * 4, :], src, AF.Identity, scale=kdc[:, 0:1])
                P.act(kdb[:, g4 * 4:(g4 + 1) * 4, :], src, AF.Identity, scale=kdc[:, 1:2])
            for c in range(16):
                pb = 6 + c % 2
                for kc in range(8):
                    P.mm(PS(pb), hT[:, kc, c * 128:(c + 1) * 128], Wvg2[:, kc, :], start=(kc == 0), stop=(kc == 7))
                P.copy('act', vtm[:, c, :], PS(pb, 0, 256))
                P.act(sgm[:, c, :], PS(pb, 256, 256), AF.Silu)
            if h + 1 < NHEADS:
                load_vg(h + 1)
        def sweeps(h):
            MbP = [Mst[:, 2, :], Mst[:, 3, :]]
            MfP = [Mst[:, 0, :], Mst[:, 1, :]]
            P.dma('sp', MbP[1], s0_d[j, 1, h], 'rs')
            P.dma('sp', MfP[0], s0_d[j, 0, h], 'rs')
            for c in range(15, -1, -1):
                cross = (c != 15) and ((c + 1) % 2 == 0)
                Mb, Mbn = MbP[c % 2], MbP[(c + 1) % 2]
                if cross:
                    P.act(SBst[:, c, :], Mb, AF.Identity, scale=flag)
                else:
                    P.copy('act', SBst[:, c, :], Mb)
                pb = 4 + c % 2
                P.mm(PS(pb, 0, 256), kdb[:, c, :], vtm[:, c, :])
                dcol = (cdf if cross else cd)[:, 8 + h:9 + h]
                P.stt('dve', Mbn, Mb, dcol, PS(pb, 0, 256), ALU.mult, ALU.add)
                if c % 2 == 0:
                    P.dma('sp', retfin_d[j, 1, h, c // 2], Mbn, 'ro', is_out=True)
            def S1(c):
                k2 = c % 2
                csl = slice(c * 128, (c + 1) * 128)
                cross = (c != 0) and (c % 2 == 0)
                Mf, Mfn = MfP[c % 2], MfP[(c + 1) % 2]
                if cross:
                    P.act(SFb[k2], Mf, AF.Identity, scale=flag)
                else:
                    P.copy('act', SFb[k2], Mf)
                P.mm(PS(k2, 0, 128), kT[:, csl], qT[:, csl])
                P.mm(PS(6 + k2, 0, 256), kdf[:, c, :], vtm[:, c, :])
                P.tt('dve', QF[k2], qT[:, csl], qd[:, 0, :], ALU.mult)
                P.tt('dve', QB[k2], qT[:, csl], qd[:, 1, :], ALU.mult)
                P.tt('dve', PT[k2], PS(k2, 0, 128), maskT, ALU.mult)
                dcol = (cdf if cross else cd)[:, h:h + 1]
                P.stt('dve', Mfn, Mf, dcol, PS(6 + k2, 0, 256), ALU.mult, ALU.add)
                if c % 2 == 1:
                    P.dma('sp', retfin_d[j, 0, h, c // 2], Mfn, 'ro', is_out=True)

            def S2(c):
                k2 = c % 2
                csl = slice(c * 128, (c + 1) * 128)
                po = PS(2 + k2, 0, 256)
                P.mm(po, PT[k2], vtm[:, c, :], start=True, stop=False)
                P.mm(po, QF[k2], SFb[k2], start=False, stop=False)
                P.mm(po, QB[k2], SBst[:, c, :], start=False, stop=True)
                P.act(junk, po, AF.Square)
                P.rsum(ssq[:, k2:k2 + 1], junk)
                P.act(ssq[:, 2 + k2:3 + k2], ssq[:, k2:k2 + 1], AF.Sqrt, bias=C('eps'), scale=1.0 / 256.0)
                P.recip(ssq[:, 2 + k2:3 + k2], ssq[:, 2 + k2:3 + k2])
                P.stt('dve', onb[k2], po, ssq[:, 2 + k2:3 + k2], gnw[:, h * 256:(h + 1) * 256], ALU.mult, ALU.mult)
                P.tt('pool', gob[k2], onb[k2], sgm[:, c, :], ALU.mult)

            def S3(c):
                k2 = c % 2
                csl = slice(c * 128, (c + 1) * 128)
                for ec in range(2):
                    P.transpose(PSB(4 + k2, ec * 128, 128), gob[k2][:, ec * 128:(ec + 1) * 128], ident)
                P.copy('act', goT[:, :, csl], PSB(4 + k2, 0, 256).rearrange("p (a i) -> p a i", i=128))

            for t in range(18):
                if t < 16:
                    S1(t)
                if 1 <= t <= 16:
                    S2(t - 1)
                if t >= 2:
                    S3(t - 2)
        def outproj(h, obanks):
            for dc in range(8):
                for tb in range(4):
                    tsl = slice(tb * 512, (tb + 1) * 512)
                    pb = obanks[(dc * 4 + tb) % len(obanks)]
                    for ec in range(2):
                        P.mm(PS(pb), Wo[:, ec, dc * 128:(dc + 1) * 128], goT[:, ec, tsl], start=(ec == 0), stop=(ec == 1))
                    P.stt('dve', xT[:, dc, tsl], PS(pb), g1[:, dc:dc + 1], xT[:, dc, tsl], ALU.mult, ALU.add)


        phaseA(0)
        for h in range(NHEADS):
            sweeps(h)
            if h + 1 < NHEADS:
                realP = P
                recO = Rec()
                P = recO
                outproj(h, [4, 5])
                recA = Rec()
                P = recA
                phaseA(h + 1)
                P = realP
                interleave(P, [recO, recA])
                load_wo(h + 1)
            else:
                outproj(h, [0, 1, 2, 3])

    if mode == 'ret':
        for c in range(8):
            P.copy('pool', hT[:, c, :], xT[:, c, :])
        P.memset('dve', mod, 0.5)
        rotary_tables()
        retention(0)
        depth = 0
    if depth > 0:
        mods(0)
    for i in range(depth):
        mod = MODS[i % 2]
        amod = AMODS[i % 2]
        norm_mod(amod[:, 0:8], mod[:, 0:8], hT_out)
        if i % 2 == 0:
            if not SKIP_HYB:
                hybrid(i // 2)
        else:
            if i == 1:
                rotary_tables()
            retention(i // 2)
        norm_mod(amod[:, 8:16], mod[:, 24:32], hT_out)
        if not SKIP_MLP:
            mlp(i, bg_layer=(i + 1 if i + 1 < depth else None))
        elif i + 1 < depth:
            mods(i + 1)
    if mode != 'ret':
        norm_mod(C('fn'), None, lambda c, tb: xT[:, c, tb * 512:(tb + 1) * 512])
    yTd3 = yT_d.rearrange("(c p) t -> p c t", p=128)
    for c in range(8):
        P.dma('sp', yTd3[:, c, :], xT[:, c, :], 'yo', is_out=True)
    P.dma('sp', s5fin_d, s5fin, 'yo', is_out=True)
    P.emit(st)
    print("ops", P.stats)
    st.close()
    return nc


def _sm(a):
    a = np.asarray(a, np.float32)
    pre = a.shape[:-2]
    a = a.reshape(pre + (16, 2, 64))
    nd = len(pre)
    a = np.transpose(a, (nd + 1, nd + 2) + tuple(range(nd)) + (nd,))
    return np.ascontiguousarray(a.reshape((128,) + pre + (16,)))


def _host_inputs(inp):
    f = lambda k: np.asarray(inp[k], np.float32)
    x_prompt, x_sample = f('x_prompt'), f('x_sample')
    common = {}
    for k in ('ada_w', 'hy_in_w', 'hy_out_w', 's5_glu_w', 'ret_in_w', 'ret_out_w', 'mlp_w1', 'mlp_w2'):
        common[k] = np.ascontiguousarray(f(k))
    common['gnw'] = np.ascontiguousarray(np.broadcast_to(f('ret_gn_w')[:, None, :], (2, 128, 2048)))
    cmat = np.zeros((128, 384), np.float32)
    cmat[:, 0:128] = np.eye(128, dtype=np.float32)
    for m in range(128):
        cmat[m ^ 32, 128 + m] = 1.0
    cmat[:, 256:384] = 1.0
    common['cmat'] = cmat
    bre = _sm(np.moveaxis(f('s5_b_re'), -1, 0))
    bim = _sm(np.moveaxis(f('s5_b_im'), -1, 0))
    cre = _sm(np.moveaxis(f('s5_c_re'), 3, 0))
    cim = _sm(np.moveaxis(f('s5_c_im'), 3, 0))
    pk = np.stack([np.transpose(a, (0, 2, 3, 4, 1)) for a in (bre, bim, cre, cim)], axis=2)
    common['s5pack'] = np.ascontiguousarray(pk.reshape(128, 2, 4, 32, 16))

    def chunked(v):
        v = np.asarray(v, np.float32)
        pre = v.shape[:-1]
        v = v.reshape(pre + (v.shape[-1] // 128, 128))
        return np.moveaxis(v, -1, 0)

    p = np.arange(128)
    base = np.zeros((128, NCP), np.float32)

    def put(name, arr):
        o, w = _CP[name]
        base[:, o:o + w] = np.asarray(arr, np.float32).reshape(128, w)

    put('n1', chunked(f('norm1_w')))
    put('n2', chunked(f('norm2_w')))
    put('fn', chunked(f('final_norm_w')))
    put('adab', chunked(f('ada_b')))
    put('s5d', chunked(f('s5_d')))
    put('glub', chunked(f('s5_glu_b')))
    put('convw', np.transpose(chunked(f('conv_w')), (0, 1, 3, 2)))
    put('convb', chunked(f('conv_b')))
    inv_freq = np.power(np.float32(10000.0), -np.arange(32, dtype=np.float32) / np.float32(32)).astype(np.float32)
    put('invf', (inv_freq[p % 32] / np.float32(TWO_PI)).astype(np.float32))
    put('sign', np.where((p % 64) < 32, -1.0, 1.0))
    put('pcol', np.stack([127.0 - p, p.astype(np.float32)], 1))
    put('lamre', _sm(f('s5_lam_re')))
    put('lamim', _sm(f('s5_lam_im')))
    put('lstep', _sm(np.broadcast_to(f('s5_log_step')[..., None], (2, 2, 32, 64))))
    put('gam', np.broadcast_to(f('ret_gamma_logit').reshape(1, 32), (128, 32)))
    put('tloc', np.broadcast_to(np.arange(256, dtype=np.float32)[None], (128, 256)))
    put('dmat', (np.arange(128)[None, :] - np.arange(128)[:, None]).astype(np.float32))
    put('ip1', np.broadcast_to(np.arange(1, 129, dtype=np.float32)[None], (128, 128)))
    put('imr', np.broadcast_to((128.0 - np.arange(128, dtype=np.float32))[None], (128, 128)))
    put('eps', np.full((128, 1), EPS, np.float32))
    t = np.arange(TOK)
    row = (t // 64).astype(np.float32)
    colp = (t % 64).astype(np.float32)
    pos_s = np.where(((p % 128) < 64)[:, None], row[None], colp[None]).astype(np.float32)
    maps = []
    for core in range(8):
        cpk = base.copy()

        def putc(name, arr):
            o, w = _CP[name]
            cpk[:, o:o + w] = np.asarray(arr, np.float32).reshape(128, w)

        m = dict(common)
        if core < 4:
            b = core
            xs = x_sample[b]
            putc('cond', chunked(f('c')[b]))
            putc('flag', np.ones((128, 1)))
            putc('h0re', _sm(f('state_s5_re')[b]))
            putc('h0im', _sm(f('state_s5_im')[b]))
            m['s0ret'] = np.ascontiguousarray(f('state_ret')[b])
            m['pos'] = pos_s
        else:
            xs = x_prompt[(core - 4) * 8:(core - 3) * 8].reshape(TOK, 1024)
            putc('cond', chunked(f('c_ctx')))
            m['s0ret'] = np.zeros((2, 2, 8, 128, 256), np.float32)
            m['pos'] = np.zeros((128, TOK), np.float32)
        m['xT'] = np.ascontiguousarray(xs.T)
        m['cpack'] = cpk
        maps.append(m)
    return maps


_NC_CACHE = {}


def kernel(**inputs):
    depth = 4
    if depth not in _NC_CACHE:
        _NC_CACHE[depth] = build(depth)
    nc = _NC_CACHE[depth]
    maps = _host_inputs(inputs)
    res = run_bass_kernel_spmd(nc, maps, core_ids=list(range(8)))
    R = res.results
    y_sample = np.stack([np.ascontiguousarray(R[b]['yT'].T) for b in range(4)], 0)
    y_prompt = np.concatenate([np.ascontiguousarray(R[c]['yT'].T).reshape(8, 256, 1024) for c in range(4, 8)], 0)
    new_re = np.zeros((32, 2, 2, 32, 64), np.float32)
    new_im = np.zeros((32, 2, 2, 32, 64), np.float32)
    new_ret = np.zeros((32, 2, 2, 8, 128, 256), np.float32)
    for c in range(4, 8):
        fin = R[c]['s5fin'].reshape(2, 64, 2, 2, 16, 8, 2)
        a = np.transpose(fin, (5, 2, 3, 4, 0, 1, 6)).reshape(8, 2, 2, 32, 64, 2)
        new_re[(c - 4) * 8:(c - 3) * 8] = a[..., 0]
        new_im[(c - 4) * 8:(c - 3) * 8] = a[..., 1]
        rf = R[c]['retfin']
        new_ret[(c - 4) * 8:(c - 3) * 8] = np.transpose(rf, (3, 0, 1, 2, 4, 5))
    return (y_prompt.astype(np.float32), y_sample.astype(np.float32), new_re, new_im, new_ret)
```
